# Optimizing a Trainium2 kernel written in Bass

```python
import math
import jax, jax.numpy as jnp
from jax import lax
import numpy as np

D_MODEL = 1024
BATCH = 16
SEQ = 2048
DEPTH = 1

D_S5 = 512
S5_GROUP = 16
S5_GROUPS = D_S5 // S5_GROUP
S5_STATE = 64
D_HY = 512
HY_ORDER = 2
HY_SHORT = 3
HY_BANDS = 16
HY_EMB = 1 + 2 * HY_BANDS
HY_HIDDEN = 64
HY_FAST_DECAY = math.log(1e-2) / 0.3
HY_SLOW_DECAY = math.log(1e-2) / 1.5
N_DIRS = 2
N_BRANCH = 2
D_FF = -(-8 * D_MODEL // (3 * 256)) * 256
D_IN = D_S5 + (HY_ORDER + 1) * D_HY + N_BRANCH * D_MODEL
EPS = 1e-6

kernel_name = "hybrid_s5_hyena_gated_encoder_block"


def _rmsnorm(x, g):
    xf = x.astype(jnp.float32)
    r = xf * lax.rsqrt(jnp.mean(xf * xf, axis=-1, keepdims=True) + EPS)
    return (r * g.astype(jnp.float32)).astype(x.dtype)


def _modulate(h, shift, scale):
    return h * (1.0 + scale[:, None, :]) + shift[:, None, :]


def _s5_scan(u, lam_re, lam_im, log_step, b_re, b_im, c_re, c_im, reverse):
    bsz, seq, _ = u.shape
    lam_re = lam_re.astype(jnp.float32); lam_im = lam_im.astype(jnp.float32)
    b_re = b_re.astype(jnp.float32); b_im = b_im.astype(jnp.float32)
    c_re = c_re.astype(jnp.float32); c_im = c_im.astype(jnp.float32)
    step = jnp.exp(log_step.astype(jnp.float32))[:, None]
    mag = jnp.exp(lam_re * step)
    abar_re = mag * jnp.cos(lam_im * step)
    abar_im = mag * jnp.sin(lam_im * step)
    num_re = abar_re - 1.0
    den = lam_re * lam_re + lam_im * lam_im
    coef_re = (num_re * lam_re + abar_im * lam_im) / den
    coef_im = (abar_im * lam_re - num_re * lam_im) / den
    bb_re = coef_re[..., None] * b_re - coef_im[..., None] * b_im
    bb_im = coef_re[..., None] * b_im + coef_im[..., None] * b_re
    ug = u.reshape(bsz, seq, S5_GROUPS, S5_GROUP)
    bu_re = jnp.einsum('blgc,gpc->lbgp', ug, bb_re)
    bu_im = jnp.einsum('blgc,gpc->lbgp', ug, bb_im)
    a_re = jnp.broadcast_to(abar_re[None, None], (seq, 1, S5_GROUPS, S5_STATE))
    a_im = jnp.broadcast_to(abar_im[None, None], (seq, 1, S5_GROUPS, S5_STATE))

    def combine(left, right):
        ar1, ai1, br1, bi1 = left
        ar2, ai2, br2, bi2 = right
        return (ar2 * ar1 - ai2 * ai1,
                ar2 * ai1 + ai2 * ar1,
                ar2 * br1 - ai2 * bi1 + br2,
                ar2 * bi1 + ai2 * br1 + bi2)

    _, _, s_re, s_im = lax.associative_scan(combine, (a_re, a_im, bu_re, bu_im), reverse=reverse, axis=0)
    y = jnp.einsum('lbgp,gcp->blgc', s_re, c_re) - jnp.einsum('lbgp,gcp->blgc', s_im, c_im)
    return y.reshape(bsz, seq, D_S5)


def _s5_branch(u, lam_re, lam_im, log_step, b_re, b_im, c_re, c_im, d, glu_w, glu_b):
    uf = u.astype(jnp.float32)
    y = uf * d.astype(jnp.float32)
    for direction in range(N_DIRS):
        y = y + _s5_scan(uf, lam_re[direction], lam_im[direction], log_step[direction],
                         b_re[direction], b_im[direction], c_re[direction], c_im[direction],
                         reverse=(direction == 1))
    z = jax.nn.gelu(y.astype(u.dtype))
    return z * jax.nn.sigmoid(z @ glu_w + glu_b)


def _short_conv(u, w, b):
    seq = u.shape[1]
    half = HY_SHORT // 2
    up = jnp.pad(u, ((0, 0), (half, HY_SHORT - 1 - half), (0, 0)))
    y = b
    for k in range(HY_SHORT):
        y = y + up[:, k:k + seq] * w[k]
    return y


def _hyena_filter_spectra(seq, w1, b1, w2, b2, w3, b3, freq, decay):
    t = jnp.arange(seq, dtype=jnp.float32)
    t01 = t / max(seq - 1, 1)
    bands = jnp.linspace(1e-4, HY_BANDS - 1, HY_BANDS, dtype=jnp.float32)
    ang = (2.0 * math.pi) * t[:, None] * bands[None, :] / seq
    feats = jnp.concatenate([t01[:, None], jnp.cos(ang), jnp.sin(ang)], axis=-1)
    f = freq.astype(jnp.float32)
    h = jnp.sin(f * (feats @ w1.astype(jnp.float32) + b1.astype(jnp.float32)))
    h = jnp.sin(f * (h @ w2.astype(jnp.float32) + b2.astype(jnp.float32)))
    h = h @ w3.astype(jnp.float32) + b3.astype(jnp.float32)
    h = h * jnp.exp(-t01[:, None] * jnp.abs(decay.astype(jnp.float32)))
    h = h.reshape(seq, HY_ORDER, N_DIRS, D_HY)
    fwd = h[:, :, 0]
    bwd = h[1:, :, 1]
    circ = jnp.concatenate([fwd, jnp.zeros((1, HY_ORDER, D_HY), jnp.float32), bwd[::-1]], axis=0)
    return jnp.fft.rfft(circ, axis=0)


def _fft_conv(z, kf, bias):
    seq = z.shape[1]
    zf32 = z.astype(jnp.float32)
    zf = jnp.fft.rfft(zf32, n=2 * seq, axis=1)
    y = jnp.fft.irfft(zf * kf[None], n=2 * seq, axis=1)[:, :seq]
    return (y + zf32 * bias.astype(jnp.float32)).astype(z.dtype)


def _hyena_branch(u, conv_w, conv_b, w1, b1, w2, b2, w3, b3, freq, decay, bias):
    seq = u.shape[1]
    u = _short_conv(u, conv_w, conv_b)
    v = u[..., :D_HY]
    gates = (u[..., D_HY:2 * D_HY], u[..., 2 * D_HY:])
    kf = _hyena_filter_spectra(seq, w1, b1, w2, b2, w3, b3, freq, decay)
    z = v
    for o in range(HY_ORDER):
        z = gates[o] * _fft_conv(z, kf[:, o], bias[o])
    return z


def setup_inputs(seed: int = 0) -> dict:
    key = jax.random.key(seed)
    ks = iter(jax.random.split(key, 48))

    def nrm(shape, std):
        return std * jax.random.normal(next(ks), shape, jnp.float32)

    G, P = S5_GROUPS, S5_STATE
    decay_init = jnp.tile(jnp.linspace(HY_FAST_DECAY, HY_SLOW_DECAY, D_HY, dtype=jnp.float32), HY_ORDER * N_DIRS)
    return {
        "x": nrm((BATCH, SEQ, D_MODEL), 1.0),
        "c": nrm((BATCH, D_MODEL), 1.0),
        "ada_w": nrm((DEPTH, D_MODEL, 6 * D_MODEL), 0.5 * D_MODEL ** -0.5),
        "ada_b": nrm((DEPTH, 6 * D_MODEL), 0.02),
        "norm1_g": 1.0 + nrm((DEPTH, D_MODEL), 0.02),
        "norm2_g": 1.0 + nrm((DEPTH, D_MODEL), 0.02),
        "w_in": nrm((DEPTH, D_MODEL, D_IN), D_MODEL ** -0.5),
        "s5_lam_re": -0.5 + nrm((DEPTH, N_DIRS, G, P), 0.01),
        "s5_lam_im": math.pi * jnp.arange(P, dtype=jnp.float32) + nrm((DEPTH, N_DIRS, G, P), 0.01),
        "s5_log_step": jax.random.uniform(next(ks), (DEPTH, N_DIRS, G), jnp.float32, math.log(1e-3), math.log(1e-1)),
        "s5_b_re": nrm((DEPTH, N_DIRS, G, P, S5_GROUP), (2 * S5_GROUP) ** -0.5),
        "s5_b_im": nrm((DEPTH, N_DIRS, G, P, S5_GROUP), (2 * S5_GROUP) ** -0.5),
        "s5_c_re": nrm((DEPTH, N_DIRS, G, S5_GROUP, P), (2 * P) ** -0.5),
        "s5_c_im": nrm((DEPTH, N_DIRS, G, S5_GROUP, P), (2 * P) ** -0.5),
        "s5_d": nrm((DEPTH, D_S5), 1.0),
        "s5_glu_w": nrm((DEPTH, D_S5, D_S5), D_S5 ** -0.5),
        "s5_glu_b": nrm((DEPTH, D_S5), 0.02),
        "hy_conv_w": nrm((DEPTH, HY_SHORT, (HY_ORDER + 1) * D_HY), HY_SHORT ** -0.5),
        "hy_conv_b": nrm((DEPTH, (HY_ORDER + 1) * D_HY), 0.02),
        "hy_ffn_w1": nrm((DEPTH, HY_EMB, HY_HIDDEN), HY_EMB ** -0.5),
        "hy_ffn_b1": nrm((DEPTH, HY_HIDDEN), 0.1),
        "hy_ffn_w2": nrm((DEPTH, HY_HIDDEN, HY_HIDDEN), HY_HIDDEN ** -0.5),
        "hy_ffn_b2": nrm((DEPTH, HY_HIDDEN), 0.1),
        "hy_ffn_w3": nrm((DEPTH, HY_HIDDEN, HY_ORDER * N_DIRS * D_HY), 0.005),
        "hy_ffn_b3": nrm((DEPTH, HY_ORDER * N_DIRS * D_HY), 0.001),
        "hy_freq": 1.0 + nrm((DEPTH, HY_HIDDEN), 0.01),
        "hy_decay": decay_init + nrm((DEPTH, HY_ORDER * N_DIRS * D_HY), 0.01),
        "hy_bias": nrm((DEPTH, HY_ORDER, D_HY), 1.0),
        "w_branch_a": nrm((DEPTH, D_S5, D_MODEL), D_S5 ** -0.5),
        "w_branch_b": nrm((DEPTH, D_HY, D_MODEL), D_HY ** -0.5),
        "w_out": nrm((DEPTH, D_MODEL, D_MODEL), D_MODEL ** -0.5),
        "ffn_w_gu": nrm((DEPTH, D_MODEL, 2 * D_FF), D_MODEL ** -0.5),
        "ffn_w_down": nrm((DEPTH, D_FF, D_MODEL), D_FF ** -0.5),
        "final_g": 1.0 + nrm((D_MODEL,), 0.02),
    }


def reference(x, c, ada_w, ada_b, norm1_g, norm2_g, w_in,
              s5_lam_re, s5_lam_im, s5_log_step, s5_b_re, s5_b_im, s5_c_re, s5_c_im,
              s5_d, s5_glu_w, s5_glu_b,
              hy_conv_w, hy_conv_b, hy_ffn_w1, hy_ffn_b1, hy_ffn_w2, hy_ffn_b2,
              hy_ffn_w3, hy_ffn_b3, hy_freq, hy_decay, hy_bias,
              w_branch_a, w_branch_b, w_out, ffn_w_gu, ffn_w_down, final_g):
    bsz, seq, _ = x.shape
    c_act = jax.nn.silu(c)
    for i in range(DEPTH):
        mod = c_act @ ada_w[i] + ada_b[i]
        sh1, sc1, g1, sh2, sc2, g2 = jnp.split(mod, 6, axis=-1)

        h = _modulate(_rmsnorm(x, norm1_g[i]), sh1, sc1)
        p = h @ w_in[i]
        u_s5 = p[..., :D_S5]
        u_hy = p[..., D_S5:D_S5 + (HY_ORDER + 1) * D_HY]
        gate = jax.nn.sigmoid(p[..., D_S5 + (HY_ORDER + 1) * D_HY:].reshape(bsz, seq, N_BRANCH, D_MODEL))
        y_a = _s5_branch(u_s5, s5_lam_re[i], s5_lam_im[i], s5_log_step[i], s5_b_re[i], s5_b_im[i],
                         s5_c_re[i], s5_c_im[i], s5_d[i], s5_glu_w[i], s5_glu_b[i]) @ w_branch_a[i]
        y_b = _hyena_branch(u_hy, hy_conv_w[i], hy_conv_b[i], hy_ffn_w1[i], hy_ffn_b1[i], hy_ffn_w2[i],
                            hy_ffn_b2[i], hy_ffn_w3[i], hy_ffn_b3[i], hy_freq[i], hy_decay[i],
                            hy_bias[i]) @ w_branch_b[i]
        merged = gate[:, :, 0] * y_a + gate[:, :, 1] * y_b
        x = x + g1[:, None, :] * (merged @ w_out[i])

        h = _modulate(_rmsnorm(x, norm2_g[i]), sh2, sc2)
        gu = h @ ffn_w_gu[i]
        x = x + g2[:, None, :] * ((jax.nn.silu(gu[..., :D_FF]) * gu[..., D_FF:]) @ ffn_w_down[i])
    return _rmsnorm(x, final_g)
```

```python
import math
from contextlib import ExitStack

import numpy as np
import ml_dtypes

import concourse.bass as bass
import concourse.mybir as mybir
from concourse.bass_utils import run_bass_kernel_spmd

F32 = mybir.dt.float32
BF16 = mybir.dt.bfloat16
U8 = mybir.dt.uint8
AF = mybir.ActivationFunctionType
ALU = mybir.AluOpType
AX = mybir.AxisListType

ENGS = ["sync", "scalar", "vector", "gpsimd", "tensor"]
PI = math.pi
TWO_PI = 2.0 * math.pi

D = 1024
DS5 = 512
DHY = 512
DFF = 2816
NF = DFF // 128
EPS = 1e-6


class Prog:
    NDSEM = 12

    def __init__(self, nc):
        self.nc = nc
        self.ops = []
        self.lastw = {}
        self.readers = {}
        self.last_on = {e: None for e in ENGS}
        self.dma_rr = {e: 0 for e in ENGS}
        self.dma_last = {}

    def op(self, eng, fn, reads=(), writes=(), dma=False):
        deps = set()
        for r in reads:
            w = self.lastw.get(r)
            if w is not None:
                deps.add(w)
        for w_ in writes:
            w = self.lastw.get(w_)
            if w is not None:
                deps.add(w)
            for rd in self.readers.get(w_, {}).values():
                deps.add(rd)
        oid = len(self.ops)
        slot = None
        if dma:
            slot = self.dma_rr[eng] % self.NDSEM
            self.dma_rr[eng] += 1
            prev = self.dma_last.get((eng, slot))
            if prev is not None:
                deps.add(prev)
            self.dma_last[(eng, slot)] = oid
        self.ops.append(dict(eng=eng, fn=fn, deps=sorted(deps), dma=dma, slot=slot, marked=False))
        for r in reads:
            self.readers.setdefault(r, {})[eng if not dma else ("dma", oid)] = oid
        for w_ in writes:
            self.lastw[w_] = oid
            self.readers[w_] = {}
        self.last_on[eng] = oid
        return oid

    def mark(self, name):
        cnt = {e: 0 for e in ENGS}
        for o in self.ops:
            if o["fn"] is not None:
                cnt[o["eng"]] += 1
        self.marks = getattr(self, "marks", [])
        self.marks.append((name, cnt))

    def barrier(self):
        lasts = set()
        for e in ENGS:
            if self.last_on[e] is not None:
                lasts.add(self.last_on[e])
        for i, o in enumerate(self.ops):
            if o["dma"] and not o.get("barriered"):
                lasts.add(i)
                o["barriered"] = True
        for e in ENGS:
            oid = len(self.ops)
            self.ops.append(dict(eng=e, fn=None, deps=sorted(lasts), dma=False, slot=None, marked=False))
            self.last_on[e] = oid
        self.lastw = {}
        self.readers = {}

    def emit(self):
        nc = self.nc
        ops = self.ops
        for o in ops:
            for d in o["deps"]:
                do = ops[d]
                if do["eng"] == "tensor" and o["eng"] == "tensor" and not do["dma"] and not o["dma"]:
                    continue
                do["marked"] = True
        cnt = {e: 0 for e in ENGS}
        dcnt = {}
        for o in ops:
            if o["dma"]:
                k = (o["eng"], o["slot"])
                dcnt[k] = dcnt.get(k, 0) + 16
                o["tok"] = (("d",) + k, dcnt[k])
            elif o["marked"] and o["fn"] is not None:
                cnt[o["eng"]] += 1
                o["tok"] = (("c", o["eng"]), cnt[o["eng"]])
            else:
                o["tok"] = None
        with ExitStack() as st:
            sems = {}
            for e in ENGS:
                sems[("c", e)] = st.enter_context(nc.semaphore("c_" + e))
                for s in range(self.NDSEM):
                    if self.dma_rr[e] > s:
                        sems[("d", e, s)] = st.enter_context(nc.semaphore("d_%s_%d" % (e, s)))
            block = st.enter_context(nc.Block())
            per_eng = {e: [o for o in ops if o["eng"] == e] for e in ENGS}

            def run(e, h):
                seen = {}
                for o in per_eng[e]:
                    for d in o["deps"]:
                        do = ops[d]
                        if do["tok"] is None:
                            continue
                        if do["eng"] == "tensor" and e == "tensor" and not do["dma"] and not o["dma"]:
                            continue
                        key, val = do["tok"]
                        if seen.get(key, 0) >= val:
                            continue
                        seen[key] = val
                        h.wait_ge(sems[key], val)
                    if o["fn"] is None:
                        continue
                    ins = o["fn"](h)
                    if o["dma"]:
                        ins.then_inc(sems[o["tok"][0]], 16)
                    elif o["tok"] is not None:
                        ins.then_inc(sems[o["tok"][0]], 1)
                for (k, v) in dcnt.items():
                    if k[0] == e:
                        key = ("d",) + k
                        if seen.get(key, 0) < v:
                            h.wait_ge(sems[key], v)

            @block.sync
            def _(h):
                run("sync", h)

            @block.scalar
            def _(h):
                run("scalar", h)

            @block.vector
            def _(h):
                run("vector", h)

            @block.gpsimd
            def _(h):
                run("gpsimd", h)

            @block.tensor
            def _(h):
                run("tensor", h)


class Arena:
    def __init__(self, tensor, size):
        self.t = tensor
        self.size = size
        self.top = 0

    def alloc(self, free_shape, dtype):
        esz = 4 if dtype == F32 else 2
        n = esz
        for s in free_shape:
            n *= s
        off = (self.top + 63) // 64 * 64
        assert off + n <= self.size, ("SBUF arena overflow", off, n, self.size)
        self.top = off + n
        ap = self.t[:, off:off + n].bitcast(dtype)
        if len(free_shape) == 2:
            ap = ap.rearrange("p (a b) -> p a b", a=free_shape[0])
        elif len(free_shape) == 3:
            ap = ap.rearrange("p (a b c) -> p a b c", a=free_shape[0], b=free_shape[1])
        return ap

    def mark(self):
        return self.top

    def release(self, m):
        self.top = m


def build(nc, L, NB, arena_bytes=200 * 1024):
    TT = L // 128
    NJ = L // 8
    W = min(512, L)
    NTB = L // W
    KT = L // 128

    def din(name, shape, dt=F32):
        return nc.dram_tensor(name, list(shape), dt, kind="ExternalInput")

    x_h = din("x", [NB, L, D])
    cT_h = din("cT", [128, 8, NB])
    ada_w_h = din("ada_w", [D, 6 * D])
    ada_b_h = din("ada_b", [1, 6 * D])
    n1g_h = din("norm1_g", [1, D])
    n2g_h = din("norm2_g", [1, D])
    fg_h = din("final_g", [1, D])
    w_in_h = din("w_in", [D, 4096])
    lamre_h = din("lamre_t", [128, 32])
    lamim_h = din("lamim_t", [128, 32])
    lstep_h = din("lstep_t", [128, 32])
    bre_h = din("bre_t", [128, 32, 16])
    bim_h = din("bim_t", [128, 32, 16])
    cre_h = din("cre_t", [128, 32, 16])
    cim_h = din("cim_t", [128, 32, 16])
    dcol_h = din("dcol", [128, 32])
    glu_w_h = din("s5_glu_w", [DS5, DS5])
    glu_b_h = din("glu_b_col", [128, 4])
    cw_h = din("conv_w_col", [128, 12, 3])
    cb_h = din("conv_b_col", [128, 12])
    hw1_h = din("hy_w1", [33, 64])
    hb1_h = din("hy_b1", [64, 1])
    hw2_h = din("hy_w2", [64, 64])
    hb2_h = din("hy_b2", [64, 1])
    hw3_h = din("hy_w3", [64, 2048])
    hb3_h = din("hy_b3", [1, 2048])
    hfr_h = din("hy_freq", [64, 1])
    hdec_h = din("hy_decay", [1, 2048])
    hbias_h = din("hy_bias", [1, 1024])
    wa_h = din("w_branch_a", [DS5, D])
    wb_h = din("w_branch_b", [DHY, D])
    wout_h = din("w_out", [D, D])
    wgu_h = din("ffn_w_gu", [D, 2 * DFF])
    wdn_h = din("ffn_w_down", [DFF, D])
    idf_h = din("ident_f", [128, 128])
    idb_h = din("ident_b", [128, 128], BF16)
    wide_h = din("wide", [128, 8, 240], BF16)
    mkf_h = din("mask_f", [128, 128])
    mkb_h = din("mask_b", [128, 128])
    feats_h = din("featsT", [33, L])
    nt01_h = din("negt01", [128, TT])
    jidx_h = din("jidx", [128, NJ])
    KH = L // 256
    LH = L // 2
    Fq_h = din("Fq", [KH, 128, 4, KH, 128], BF16)
    Gq_h = din("Gq", [2, KH, 128, 2, KH, 128], BF16)
    nt01q_h = din("negt01q", [128, 2, KH])
    out_h = nc.dram_tensor("out", [NB, L, D], F32, kind="ExternalOutput")
    modrow_h = nc.dram_tensor("modrow", [NB, 6 * D], F32, kind="Internal")
    s5w_h = nc.dram_tensor("s5w", [16, 10, 128, 128], BF16, kind="Internal")
    hs_h = nc.dram_tensor("hspec", [KH, 128, 4, 1024], F32, kind="Internal")
    hts_h = nc.dram_tensor("hT_spill", [128, 8, L], BF16, kind="Internal")
    tw_h = nc.dram_tensor("twtab", [16, 128, 4, NJ], F32, kind="Internal")
    x1s_h = nc.dram_tensor("x1_spill", [L, D], F32, kind="Internal")

    st = ExitStack()
    arena_t = st.enter_context(nc.sbuf_tensor("arena", [128, arena_bytes], U8))
    AR = Arena(arena_t, arena_bytes)
    PS = [st.enter_context(nc.psum_tensor("ps%d" % i, [128, 512], F32)) for i in range(8)]
    P = Prog(nc)

    def psf(i):
        return PS[i][:]

    def psb(i):
        return PS[i][:].bitcast(BF16)

    def V(fn, r, w):
        P.op("vector", fn, r, w)

    def S(fn, r, w):
        P.op("scalar", fn, r, w)

    def T(fn, r, w):
        P.op("tensor", fn, r, w)

    def DMA(out, in_, r, w, eng="sync"):
        P.op(eng, lambda h: h.dma_start(out=out, in_=in_), r, w, dma=True)

    def bcast_rows(handle, offset, n):
        return bass.AP(handle, offset, [[0, 128], [1, n]])

    def tcopy(eng, out, in_, r, w):
        if eng == "vector":
            V(lambda h: h.tensor_copy(out=out, in_=in_), r, w)
        else:
            S(lambda h: h.activation(out=out, in_=in_, func=AF.Copy), r, w)

    def tt(out, a, b, op, r, w):
        V(lambda h: h.tensor_tensor(out=out, in0=a, in1=b, op=op), r, w)

    def ptt(out, a, b, op, r, w):
        P.op("vector", lambda h: h.tensor_tensor(out=out, in0=a, in1=b, op=op), r, w)

    def tsc(out, a, s1, s2, op0, op1, r, w):
        P.op("vector", lambda h: h.tensor_scalar(out=out, in0=a, scalar1=s1, scalar2=s2, op0=op0, op1=op1), r, w)

    MAGIC = 12582912.0
    INV2PI = 1.0 / TWO_PI

    def rr(out, x_, tmp, rk, wk, tk):
        tsc(tmp, x_, INV2PI, MAGIC, ALU.mult, ALU.add, rk, [tk])
        V(lambda h: h.tensor_single_scalar(out=tmp, in_=tmp, scalar=-MAGIC, op=ALU.add), [tk], [tk])
        stt(out, tmp, -TWO_PI, x_, ALU.mult, ALU.add, [tk] + list(rk), [wk])

    def stt(out, a, s, b, op0, op1, r, w):
        V(lambda h: h.scalar_tensor_tensor(out=out, in0=a, scalar=s, in1=b, op0=op0, op1=op1), r, w)

    def act(out, in_, func, r, w, bias=None, scale=None):
        kw = {}
        if bias is not None:
            kw["bias"] = bias
        if scale is not None:
            kw["scale"] = scale
        S(lambda h: h.activation(out=out, in_=in_, func=func, **kw), r, w)

    def mm(out, lhsT, rhs, start, stop, r, w):
        T(lambda h: h.matmul(out, lhsT, rhs, start=start, stop=stop), r, w)

    def tr(out, in_, ident, r, w):
        T(lambda h: h.transpose(out=out, in_=in_, identity=ident), r, w)

    idf = AR.alloc([128], F32)
    idb = AR.alloc([128], BF16)
    wide = AR.alloc([8, 240], BF16)
    ones_f = AR.alloc([128], F32)
    rho_t = AR.alloc([32], F32)
    psi_t = AR.alloc([32], F32)
    jidx = AR.alloc([NJ], F32)
    glu_b = AR.alloc([4], F32)
    cw = AR.alloc([12, 3], F32)
    cb = AR.alloc([12], F32)
    negpi = AR.alloc([1], F32)
    one_c = AR.alloc([1], F32)
    DMA(idf, idf_h.ap(), [], ["idf"])
    DMA(idb, idb_h.ap(), [], ["idb"])
    DMA(wide, wide_h.ap(), [], ["wide"])
    DMA(jidx, jidx_h.ap(), [], ["jidx"])
    DMA(glu_b, glu_b_h.ap(), [], ["glu_b"])
    DMA(cw, cw_h.ap(), [], ["cw"])
    DMA(cb, cb_h.ap(), [], ["cb"])
    V(lambda h: h.memset(ones_f, 1.0), [], ["ones_f"])
    V(lambda h: h.memset(one_c, 1.0), [], ["one_c"])
    V(lambda h: h.memset(negpi, 0.5 * PI), [], ["negpi"])
    P.barrier()
    P.mark("init")
    base_mark = (AR.mark() + 63) // 64 * 64
    LR = 2048
    A0 = base_mark
    B0 = A0 + 8 * LR * 2
    C0 = B0 + 4 * LR * 2
    D0 = C0 + 12 * LR * 2
    E0 = D0 + 4 * LR * 2
    Z0 = E0 + 12 * LR * 2

    def at(off):
        AR.top = off

    def rev(ap_):
        (ps_, pc_), (fs_, fc_) = ap_.ap
        return bass.AP(ap_.tensor, ap_.offset + (fc_ - 1) * fs_, [[ps_, pc_], [-fs_, fc_]])

    def gen_prepA():
        at(158 * 1024)
        cT = AR.alloc([8, NB], F32)
        scT = AR.alloc([8, NB], F32)
        adab = [AR.alloc([512], F32) for _ in range(2)]
        modsb = [AR.alloc([512], F32) for _ in range(2)]
        awb = [AR.alloc([8, 512], F32) for _ in range(2)]
        assert AR.top <= arena_bytes
        DMA(cT, cT_h.ap(), [], ["cT"])
        act(scT, cT, AF.Sigmoid, ["cT"], ["scT"])
        tt(scT, scT, cT, ALU.mult, ["scT", "cT"], ["scT"])
        adaw_v = ada_w_h.ap().rearrange("(kh kl) n -> kl kh n", kl=128)
        yield
        for blk in range(12):
            i2 = blk % 2
            wb_ = awb[i2]
            wk = "awb%d" % i2
            DMA(wb_, adaw_v[:, :, blk * 512:(blk + 1) * 512], [], [wk])
            DMA(adab[i2][0:1, :], ada_b_h.ap()[:, blk * 512:(blk + 1) * 512], [], ["adab%d" % i2])
            pk = "ps%d" % (6 + i2)
            po = psf(6 + i2)[0:NB, :]
            mm(po, ones_f[0:1, 0:NB], adab[i2][0:1, :], True, False, ["ones_f", "adab%d" % i2], [pk])
            for kh in range(8):
                mm(po, scT[:, kh, :], wb_[:, kh, :], False, kh == 7, ["scT", wk], [pk])
            tcopy("scalar", modsb[i2][0:NB, :], po, [pk], ["modsb%d" % i2])
            DMA(modrow_h.ap()[:, blk * 512:(blk + 1) * 512], modsb[i2][0:NB, :], ["modsb%d" % i2], ["modrow_%d" % blk])
            yield

    gA = gen_prepA()
    next(gA)
    at(A0)

    m0 = AR.mark()

    def sm(n=32):
        return AR.alloc([n], F32)

    lamre, lamim, lstep = sm(), sm(), sm()
    DMA(lamre, lamre_h.ap(), [], ["lamre"])
    DMA(lamim, lamim_h.ap(), [], ["lamim"])
    DMA(lstep, lstep_h.ap(), [], ["lstep"])
    bre = AR.alloc([32, 16], F32)
    bim = AR.alloc([32, 16], F32)
    cre = AR.alloc([32, 16], F32)
    cim = AR.alloc([32, 16], F32)
    dcol = sm()
    mkf = AR.alloc([128], F32)
    mkb = AR.alloc([128], F32)
    DMA(bre, bre_h.ap(), [], ["bre"])
    DMA(bim, bim_h.ap(), [], ["bim"])
    DMA(cre, cre_h.ap(), [], ["cre"])
    DMA(cim, cim_h.ap(), [], ["cim"])
    DMA(dcol, dcol_h.ap(), [], ["dcol"])
    DMA(mkf, mkf_h.ap(), [], ["mkf"])
    DMA(mkb, mkb_h.ap(), [], ["mkb"])

    _names = {}

    def key(ap_obj, name):
        _names[id(ap_obj)] = name
        return ap_obj

    def nm(a):
        return _names[id(a)]

    for a, n in [(lamre, "lamre"), (lamim, "lamim"), (lstep, "lstep")]:
        key(a, n)
    _tmpc = [0]

    def new(name=None, n=32):
        a = sm(n)
        _tmpc[0] += 1
        return key(a, name or ("t%d" % _tmpc[0]))

    def e_tt(o, a, b, op):
        tt(o, a, b, op, [nm(a), nm(b)], [nm(o)])

    def e_ts(o, a, s1, s2, op0, op1=None):
        if op1 is None:
            V(lambda h: h.tensor_single_scalar(out=o, in_=a, scalar=s1, op=op0), [nm(a)], [nm(o)])
        else:
            tsc(o, a, s1, s2, op0, op1, [nm(a)], [nm(o)])

    step = new("step")
    act(step, lstep, AF.Exp, ["lstep"], ["step"])
    aa = new("aa")
    e_tt(aa, lamre, step, ALU.mult)
    phi = new("phi")
    e_tt(phi, lamim, step, ALU.mult)

    def horner(o, xin, coefs):
        e_ts(o, xin, coefs[-1], coefs[-2], ALU.mult, ALU.add)
        for c in reversed(coefs[:-2]):
            e_tt(o, o, xin, ALU.mult)
            e_ts(o, o, float(c), None, ALU.add)

    mag = new("mag")
    horner(mag, aa, [1.0 / math.factorial(k) for k in range(13)])
    hr = new("hr")
    rrt = new("rrt")
    rr(hr, phi, rrt, ["phi"], "hr", "rrt")
    e_ts(hr, hr, 0.5, None, ALU.mult)
    hr2 = new("hr2")
    e_tt(hr2, hr, hr, ALU.mult)
    sh = new("sh")
    horner(sh, hr2, [(-1.0) ** k / math.factorial(2 * k + 1) for k in range(9)])
    e_tt(sh, sh, hr, ALU.mult)
    ch = new("ch")
    horner(ch, hr2, [(-1.0) ** k / math.factorial(2 * k) for k in range(9)])
    sinp, cosp = new("sinp"), new("cosp")
    e_tt(sinp, sh, ch, ALU.mult)
    e_ts(sinp, sinp, 2.0, None, ALU.mult)
    e_tt(cosp, sh, sh, ALU.mult)
    e_ts(cosp, cosp, -2.0, 1.0, ALU.mult, ALU.add)
    ar_, ai_ = new("ar"), new("ai")
    e_tt(ar_, mag, cosp, ALU.mult)
    e_tt(ai_, mag, sinp, ALU.mult)
    numr = new("numr")
    e_ts(numr, ar_, -1.0, None, ALU.add)
    den, t1, t2 = new("den"), new("t1"), new("t2")
    e_tt(den, lamre, lamre, ALU.mult)
    e_tt(t1, lamim, lamim, ALU.mult)
    e_tt(den, den, t1, ALU.add)
    V(lambda h: h.reciprocal(out=den, in_=den), ["den"], ["den"])
    cor, coi = new("cor"), new("coi")
    e_tt(cor, numr, lamre, ALU.mult)
    e_tt(t1, ai_, lamim, ALU.mult)
    e_tt(cor, cor, t1, ALU.add)
    e_tt(cor, cor, den, ALU.mult)
    e_tt(coi, ai_, lamre, ALU.mult)
    e_tt(t1, numr, lamim, ALU.mult)
    e_tt(coi, coi, t1, ALU.subtract)
    e_tt(coi, coi, den, ALU.mult)

    def cmul(orr, oi, a_r, a_i, b_r, b_i):
        e_tt(t1, a_r, b_r, ALU.mult)
        e_tt(t2, a_i, b_i, ALU.mult)
        e_tt(orr, t1, t2, ALU.subtract)
        e_tt(t1, a_r, b_i, ALU.mult)
        e_tt(t2, a_i, b_r, ALU.mult)
        e_tt(oi, t1, t2, ALU.add)

    pwr, pwi = [None] * 9, [None] * 9
    pwr[0], pwi[0] = new("pwr0"), new("pwi0")
    V(lambda h: h.memset(pwr[0], 1.0), [], ["pwr0"])
    V(lambda h: h.memset(pwi[0], 0.0), [], ["pwi0"])
    pwr[1], pwi[1] = ar_, ai_
    for k in range(2, 9):
        pwr[k], pwi[k] = new("pwr%d" % k), new("pwi%d" % k)
        cmul(pwr[k], pwi[k], pwr[k - 1], pwi[k - 1], ar_, ai_)
    ipr, ipi = [None] * 9, [None] * 9
    im2 = new("im2")
    e_tt(im2, mag, mag, ALU.mult)
    V(lambda h: h.reciprocal(out=im2, in_=im2), ["im2"], ["im2"])
    ipr[1], ipi[1] = new("ipr1"), new("ipi1")
    e_tt(ipr[1], ar_, im2, ALU.mult)
    e_tt(ipi[1], ai_, im2, ALU.mult)
    e_ts(ipi[1], ipi[1], -1.0, None, ALU.mult)
    for k in range(2, 9):
        ipr[k], ipi[k] = new("ipr%d" % k), new("ipi%d" % k)
        cmul(ipr[k], ipi[k], ipr[k - 1], ipi[k - 1], ipr[1], ipi[1])
    e_tt(t1, mag, mag, ALU.mult)
    e_tt(t1, t1, t1, ALU.mult)
    tt(rho_t, t1, t1, ALU.mult, ["t1"], ["rho_t"])
    e_ts(t2, phi, 8.0, None, ALU.mult)
    rr(psi_t, t2, rrt, ["t2"], "psi_t", "rrt")

    def b3(a):
        return a.unsqueeze(2).to_broadcast([128, 32, 16])

    def b3h(a, d):
        return a[:, d * 16:(d + 1) * 16].unsqueeze(2).to_broadcast([128, 16, 16])

    bbr = AR.alloc([32, 16], F32)
    bbi = AR.alloc([32, 16], F32)
    w1_ = AR.alloc([32, 16], F32)
    w2_ = AR.alloc([32, 16], F32)
    tt(w1_, bre, b3(cor), ALU.mult, ["bre", "cor"], ["w1_"])
    tt(w2_, bim, b3(coi), ALU.mult, ["bim", "coi"], ["w2_"])
    tt(bbr, w1_, w2_, ALU.subtract, ["w1_", "w2_"], ["bbr"])
    tt(w1_, bim, b3(cor), ALU.mult, ["bim", "cor"], ["w1_"])
    tt(w2_, bre, b3(coi), ALU.mult, ["bre", "coi"], ["w2_"])
    tt(bbi, w1_, w2_, ALU.add, ["w1_", "w2_"], ["bbi"])

    Bt = [AR.alloc([32, 8, 16], F32) for _ in range(2)]
    Bti = [AR.alloc([32, 8, 16], F32) for _ in range(2)]
    Cs = [AR.alloc([32, 8, 16], F32) for _ in range(2)]

    def cprod(dst_r, dst_i, sr, si, xr_, xi_, d, s, neg_im=False, names=("x", "y")):
        sl = slice(d * 16, (d + 1) * 16)
        o_r = dst_r[:, sl, s, :]
        o_i = dst_i[:, sl, s, :]
        a1 = w1_[:, sl, :]
        a2 = w2_[:, sl, :]
        rk = [nm(sr), nm(si), names[0], names[1]]
        tt(a1, xr_[:, sl, :], b3h(sr, d), ALU.mult, rk, ["w1_"])
        tt(a2, xi_[:, sl, :], b3h(si, d), ALU.mult, rk, ["w2_"])
        tt(o_r, a1, a2, ALU.subtract, ["w1_", "w2_"], [names[2]])
        tt(a1, xi_[:, sl, :], b3h(sr, d), ALU.mult, rk, ["w1_"])
        tt(a2, xr_[:, sl, :], b3h(si, d), ALU.mult, rk, ["w2_"])
        if neg_im:
            tt(o_i, a1, a2, ALU.add, ["w1_", "w2_"], [names[3]])
            V(lambda h: h.tensor_single_scalar(out=o_i, in_=o_i, scalar=-1.0, op=ALU.mult), [names[3]], [names[3]])
        else:
            tt(o_i, a1, a2, ALU.add, ["w1_", "w2_"], [names[3]])

    for d in range(2):
        for s in range(8):
            next(gA, None)
            e = 7 - s if d == 0 else s
            cprod(Bt[0], Bt[1], pwr[e], pwi[e], bbr, bbi, d, s, names=("bbr", "bbi", "Bt0", "Bt1"))
            k = s + 1 if d == 0 else 8 - s
            cprod(Bti[0], Bti[1], ipr[k], ipi[k], bbr, bbi, d, s, names=("bbr", "bbi", "Bti0", "Bti1"))
            f = s + 1 if d == 0 else 8 - s
            cprod(Cs[0], Cs[1], pwr[f], pwi[f], cre, cim, d, s, neg_im=True, names=("cre", "cim", "Cs0", "Cs1"))

    stage = [AR.alloc([10, 128], BF16) for _ in range(2)]
    gtmp = AR.alloc([128], F32)
    gtmp2 = AR.alloc([128], F32)
    for q in range(16):
        next(gA, None)
        sg = stage[q % 2]
        sk = "stage%d" % (q % 2)
        for d in range(2):
            dq = d * 16 + q
            for part in range(2):
                pk = "ps%d" % ((d * 2 + part) % 4)
                po = psf((d * 2 + part) % 4)[:, 0:128]
                src = Bt[part][:, dq, :, :].rearrange("p s c -> p (s c)")
                tr(po, src, idf, ["Bt%d" % part, "idf"], [pk])
                tcopy("scalar", sg[:, d * 2 + part, :], po, [pk], [sk])
                csrc = Cs[part][:, dq, :, :].rearrange("p s c -> p (s c)")
                tcopy("vector", sg[:, 4 + d * 2 + part, :], csrc, ["Cs%d" % part], [sk])
        for two in range(2):
            g = 2 * q + two
            rows = slice(two * 64, (two + 1) * 64)
            for d in range(2):
                dq = d * 16 + q
                pk = "ps%d" % (4 + d)
                po = psf(4 + d)[:, 0:128]
                l0 = Bti[0][rows, dq, :, :].rearrange("p s c -> p (s c)")
                l1 = Bti[1][rows, dq, :, :].rearrange("p s c -> p (s c)")
                r0 = Cs[0][rows, dq, :, :].rearrange("p s c -> p (s c)")
                r1 = Cs[1][rows, dq, :, :].rearrange("p s c -> p (s c)")
                mm(po, l0, r0, True, False, ["Bti0", "Cs0"], [pk])
                mm(po, l1, r1, False, True, ["Bti1", "Cs1"], [pk])
            tt(gtmp, psf(4)[:, 0:128], mkf, ALU.mult, ["ps4", "mkf"], ["gtmp"])
            tt(gtmp2, psf(5)[:, 0:128], mkb, ALU.mult, ["ps5", "mkb"], ["gtmp2"])
            tt(gtmp, gtmp, gtmp2, ALU.add, ["gtmp", "gtmp2"], ["gtmp"])
            stt(sg[:, 8 + two, :], idf, dcol[:, g:g + 1], gtmp, ALU.mult, ALU.add, ["idf", "dcol", "gtmp"], [sk])
        DMA(s5w_h.ap()[q].rearrange("m p n -> p m n"), sg, [sk], ["s5w_%d" % q])
    for _ in gA:
        pass
    assert AR.top <= 158 * 1024, AR.top
    P.barrier()
    P.mark("prepB_s5")
    at(A0)

    m0 = AR.mark()
    hw1 = AR.alloc([64], F32)
    hw2 = AR.alloc([64], F32)
    hw3 = AR.alloc([2048], F32)
    hb1 = AR.alloc([1], F32)
    hb2 = AR.alloc([1], F32)
    hfr = AR.alloc([1], F32)
    hb3 = AR.alloc([2048], F32)
    adec = AR.alloc([2048], F32)
    hbias = AR.alloc([1024], F32)
    nt01 = AR.alloc([2, KH], F32)
    feats = AR.alloc([L], F32)
    DMA(hw1[0:33, :], hw1_h.ap(), [], ["hw1"])
    DMA(hw2[0:64, :], hw2_h.ap(), [], ["hw2"])
    DMA(hw3[0:64, :], hw3_h.ap(), [], ["hw3"])
    DMA(hb1[0:64, :], hb1_h.ap(), [], ["hb1"])
    DMA(hb2[0:64, :], hb2_h.ap(), [], ["hb2"])
    DMA(hfr[0:64, :], hfr_h.ap(), [], ["hfr"])
    DMA(hw3[64:65, :], hb3_h.ap(), [], ["hw3"])
    DMA(adec, bcast_rows(hdec_h, 0, 2048), [], ["adec"])
    DMA(hbias, bcast_rows(hbias_h, 0, 1024), [], ["hbias"])
    DMA(nt01, nt01q_h.ap(), [], ["nt01"])
    DMA(feats[0:33, :], feats_h.ap(), [], ["feats"])
    act(adec, adec, AF.Abs, ["adec"], ["adec"])
    tg_a = AR.alloc([NJ], F32)
    tg_m = AR.alloc([NJ], F32)
    tg_r = AR.alloc([NJ], F32)
    tg_s = [AR.alloc([4, NJ], F32) for _ in range(2)]
    for q in range(16):
        sg_ = tg_s[q % 2]
        sgk = "tg_s%d" % (q % 2)
        for d in range(2):
            dq = d * 16 + q
            V(lambda h, dq=dq: h.tensor_scalar_mul(out=tg_a, in0=jidx, scalar1=psi_t[:, dq:dq + 1]),
              ["jidx", "psi_t"], ["tg_a"])
            rr(tg_r, tg_a, tg_m, ["tg_a"], "tg_r", "tg_m")
            act(sg_[:, 2 * d, :], tg_r, AF.Sin, ["tg_r"], [sgk], scale=0.999998)
            act(tg_r, tg_r, AF.Abs, ["tg_r"], ["tg_r"])
            act(sg_[:, 2 * d + 1, :], tg_r, AF.Sin, ["tg_r", "negpi"], [sgk], bias=negpi[:, 0:1], scale=-1.0)
        DMA(tw_h.ap()[q], sg_, [sgk], ["twtab_%d" % q])
    h1 = AR.alloc([L], F32)
    h2 = AR.alloc([L], F32)
    argt = AR.alloc([W], F32)
    argr = AR.alloc([W], F32)
    argm = AR.alloc([W], F32)
    for (src, wmat, bcol, dst, sk_, wk_, bk_, dk_, kk) in [
        (feats, hw1, hb1, h1, "feats", "hw1", "hb1", "h1", 33),
        (h1, hw2, hb2, h2, "h1", "hw2", "hb2", "h2", 64),
    ]:
        for tb in range(NTB):
            po = psf(tb % 2)[0:64, 0:W]
            pk = "ps%d" % (tb % 2)
            mm(po, wmat[0:kk, 0:64], src[0:kk, tb * W:(tb + 1) * W], True, True, [wk_, sk_], [pk])
            a_ = argt[0:64, :]
            tsc(a_, po, bcol[0:64, 0:1], hfr[0:64, 0:1], ALU.add, ALU.mult, [pk, bk_, "hfr"], ["argt"])
            rr(argr[0:64, :], a_, argm[0:64, :], ["argt"], "argr", "argm")
            act(dst[0:64, tb * W:(tb + 1) * W], argr[0:64, :], AF.Sin, ["argr"], [dk_], scale=0.999998)
    V(lambda h: h.memset(h2[64:65, :], 1.0), [], ["h2"])
    ET = AR.alloc([2, KH, 1024], BF16)
    OT = AR.alloc([2, KH, 1024], BF16)
    hfw = AR.alloc([512], F32)
    hbw = AR.alloc([512], F32)
    dect = AR.alloc([512], F32)
    for r_ in range(2):
        for mh in range(KH):
            c0 = 2 * mh * 128 + r_
            h2s = h2[0:65, c0:c0 + 255:2]
            for o in range(2):
                for dr in range(2):
                    cbk = o * 2 + dr
                    cols = slice(cbk * 512, (cbk + 1) * 512)
                    pi_ = 2 + dr
                    pk = "ps%d" % pi_
                    po = psf(pi_)
                    mm(po, h2s, hw3[0:65, cols], True, True, ["h2", "hw3"], [pk])
                    act(dect, adec[:, cols], AF.Exp, ["adec", "nt01"], ["dect"], scale=nt01[:, r_, mh:mh + 1])
                    dst = hfw if dr == 0 else hbw
                    tt(dst, po, dect, ALU.mult, [pk, "dect"], ["hfw" if dr == 0 else "hbw"])
                if r_ == 0 and mh == 0:
                    V(lambda h: h.memset(hbw[0:1, :], 0.0), ["hbw"], ["hbw"])
                tt(ET[:, r_, mh, o * 512:(o + 1) * 512], hfw, hbw, ALU.add, ["hfw", "hbw"], ["ET"])
                tt(OT[:, r_, mh, o * 512:(o + 1) * 512], hfw, hbw, ALU.subtract, ["hfw", "hbw"], ["OT"])
    fblk = [AR.alloc([4, KH, 128], BF16) for _ in range(2)]
    hst = [AR.alloc([4, 512], F32) for _ in range(2)]
    tA = [AR.alloc([512], F32) for _ in range(2)]
    tB = [AR.alloc([512], F32) for _ in range(2)]
    tC = [AR.alloc([512], F32) for _ in range(2)]
    for kt in range(KH):
        bb_ = kt % 2
        fk = "fblk%d" % bb_
        DMA(fblk[bb_], Fq_h.ap()[kt], [], [fk])
        for hh in range(2):
            u_ = (kt * 2 + hh) % 2
            cs = slice(hh * 512, (hh + 1) * 512)
            for X in range(4):
                srcT = ET if X < 2 else OT
                r_ = X % 2
                pi_ = 4 * u_ + X
                for mh in range(KH):
                    mm(psf(pi_), fblk[bb_][:, X, mh, :], srcT[:, r_, mh, cs], mh == 0, mh == KH - 1,
                       [fk, "ET" if X < 2 else "OT"], ["ps%d" % pi_])
            pe, po_, pbe, pbo = [4 * u_ + X for X in range(4)]
            hk = "hst%d" % u_
            tcopy("scalar", tA[u_], psf(po_), ["ps%d" % po_], ["tA%d" % u_])
            tt(tB[u_], psf(pe), hbias[:, cs], ALU.add, ["ps%d" % pe, "hbias"], ["tB%d" % u_])
            tt(hst[u_][:, 0, :], tB[u_], tA[u_], ALU.add, ["tA%d" % u_, "tB%d" % u_], [hk])
            tt(hst[u_][:, 2, :], tB[u_], tA[u_], ALU.subtract, ["tA%d" % u_, "tB%d" % u_], [hk])
            tcopy("scalar", tC[u_], psf(pbo), ["ps%d" % pbo], ["tC%d" % u_])
            tt(hst[u_][:, 1, :], psf(pbe), tC[u_], ALU.add, ["ps%d" % pbe, "tC%d" % u_], [hk])
            tt(hst[u_][:, 3, :], tC[u_], psf(pbe), ALU.subtract, ["ps%d" % pbe, "tC%d" % u_], [hk])
            DMA(hs_h.ap()[kt][:, :, cs], hst[u_], [hk], ["hspec_%d_%d" % (kt, hh)], eng="scalar" if u_ else "sync")
    P.barrier()
    P.mark("prepC_hy")
    AR.release(m0)

    win_v = w_in_h.ap().rearrange("(kh kl) n -> kl kh n", kl=128)

    def load_row(dst, b, idx, kname):
        DMA(dst, bcast_rows(modrow_h, b * 6 * D + idx * D, D), ["modrow_%d" % k_ for k_ in range(12)], [kname])

    def rms_rstd(xt, xk, sq, ss, tag):
        act(sq, xt, AF.Square, [xk], ["sq" + tag])
        V(lambda h: h.reduce_sum(out=ss, in_=sq, axis=AX.X), ["sq" + tag], ["ss" + tag])
        tsc(ss, ss, 1.0 / D, EPS, ALU.mult, ALU.add, ["ss" + tag], ["ss" + tag])
        act(ss, ss, AF.Sqrt, ["ss" + tag], ["ss" + tag])
        V(lambda h: h.reciprocal(out=ss, in_=ss), ["ss" + tag], ["ss" + tag])

    for b in range(NB):
        at(A0)
        hT = AR.alloc([8, L], BF16)
        at(B0)
        us5 = AR.alloc([4, L], BF16)
        at(C0)
        uhy = AR.alloc([12, L], BF16)
        at(D0)
        rowA = AR.alloc([D], F32)
        rowB = AR.alloc([D], F32)
        rowg = AR.alloc([D], F32)
        load_row(rowB, b, 0, "rowB")
        load_row(rowA, b, 1, "rowA")
        DMA(rowg, bcast_rows(n1g_h, 0, D), [], ["rowg"])
        stt(rowA, rowA, 1.0, rowg, ALU.add, ALU.mult, ["rowA", "rowg"], ["rowA"])
        xt = [AR.alloc([D], F32) for _ in range(3)]
        sqs = [AR.alloc([D], F32) for _ in range(3)]
        sqm = [AR.alloc([D], F32) for _ in range(2)]
        hb_ = [AR.alloc([D], BF16) for _ in range(2)]
        ss = [AR.alloc([1], F32) for _ in range(3)]

        def p1_s0(t_):
            i3 = t_ % 3
            xk = "xt%d" % i3
            DMA(xt[i3], x_h.ap()[b, t_ * 128:(t_ + 1) * 128, :], [], [xk])
            rms_rstd(xt[i3], xk, sqs[i3], ss[i3], "a%d" % i3)

        def p1_s1(t_):
            i2 = t_ % 2
            i3 = t_ % 3
            xk = "xt%d" % i3
            stt(sqm[i2], xt[i3], ss[i3][:, 0:1], rowA, ALU.mult, ALU.mult, [xk, "ssa%d" % i3, "rowA"], ["sqm%d" % i2])
            ptt(hb_[i2], sqm[i2], rowB, ALU.add, ["sqm%d" % i2, "rowB"], ["hb%d" % i2])

        def p1_s2(t_):
            i2 = t_ % 2
            pk = "ps%d" % i2
            pv = psb(i2).rearrange("p (a b) -> p a b", a=8)
            for dc in range(8):
                tr(pv[:, dc, :], hb_[i2][:, dc * 128:(dc + 1) * 128], idb, ["hb%d" % i2, "idb"], [pk])
            tcopy("scalar", hT[:, :, t_ * 128:(t_ + 1) * 128], pv, [pk], ["hT"])

        p1_s0(0)
        if TT > 1:
            p1_s0(1)
        p1_s1(0)
        for t_ in range(TT):
            if t_ + 2 < TT:
                p1_s0(t_ + 2)
            if t_ + 1 < TT:
                p1_s1(t_ + 1)
            p1_s2(t_)
        DMA(hts_h.ap(), hT, ["hT"], ["hts"])
        P.mark("b%d_p1_norm" % b)
        wst = [AR.alloc([8, 512], BF16) for _ in range(2)]
        for mt in range(4):
            wi = (mt // 4) % 2
            wk = "wst%d" % wi
            if mt % 4 == 0:
                DMA(wst[wi], win_v[:, :, mt * 128:(mt + 4) * 128], [], [wk], eng="gpsimd")
            for tb in range(NTB):
                pi_ = 2 + (mt * NTB + tb) % 4
                pk = "ps%d" % pi_
                po = psf(pi_)[:, 0:W]
                for kc in range(8):
                    mm(po, wst[wi][:, kc, (mt % 4) * 128:(mt % 4 + 1) * 128], hT[:, kc, tb * W:(tb + 1) * W], kc == 0, kc == 7,
                       [wk, "hT"], [pk])
                if mt < 4:
                    dst, dk = us5[:, mt, tb * W:(tb + 1) * W], "us5_%d" % mt
                else:
                    dst, dk = uhy[:, mt - 4, tb * W:(tb + 1) * W], "uhy_%d" % (mt - 4)
                tcopy("vector" if (mt + tb) % 2 else "scalar", dst, po, [pk], [dk])
        P.barrier()
        P.mark("b%d_p1_win" % b)

        at(D0)
        s5o = AR.alloc([4, L], BF16)
        at(D0)
        wst2 = [AR.alloc([8, 128], BF16) for _ in range(2)]
        at(E0)
        wq = [AR.alloc([10, 128], BF16) for _ in range(3)]
        uf = [AR.alloc([2, NJ], BF16) for _ in range(3)]
        XL = [[[AR.alloc([NJ], F32) for _ in range(2)] for _ in range(2)] for _ in range(2)]
        Xb = [[[AR.alloc([NJ], BF16) for _ in range(2)] for _ in range(2)] for _ in range(2)]
        XT = [[AR.alloc([NJ], F32) for _ in range(2)] for _ in range(2)]
        twt = [AR.alloc([4, NJ], F32) for _ in range(2)]
        tws = [[twt[i][:, 2 * d, :] for d in range(2)] for i in range(2)]
        twc = [[twt[i][:, 2 * d + 1, :] for d in range(2)] for i in range(2)]
        ang = [AR.alloc([NJ], F32) for _ in range(2)]
        angt = [AR.alloc([NJ], F32) for _ in range(2)]
        angm = [AR.alloc([NJ], F32) for _ in range(2)]
        tmp4 = [[AR.alloc([NJ], F32) for _ in range(4)] for _ in range(2)]
        zfold = [AR.alloc([8, NJ], BF16) for _ in range(2)]
        gq = [AR.alloc([NJ], F32) for _ in range(2)]
        gs = [AR.alloc([NJ], F32) for _ in range(2)]
        us5s = AR.alloc([2, 8, NJ], BF16)

        def s_major(chn):
            P.op("gpsimd", lambda h, chn=chn, us5s=us5s, us5=us5: h.tensor_copy(
                out=us5s[:, chn % 2, :, :], in_=us5[:, chn, :].rearrange("p (j s) -> p s j", s=8)),
                ["us5_%d" % chn], ["us5s_%d" % (chn % 2)])

        s_major(0)
        s5w_v = s5w_h.ap()
        SINS = 0.999998

        def GP(fn, r, w):
            P.op("gpsimd", fn, r, w)

        def stageA(q):
            qi = q % 2
            q3 = q % 3
            wqi = wq[q3]
            wk = "wq%d" % q3
            chn = q // 4
            ufk = "uf%d" % q3
            DMA(wqi, s5w_v[q].rearrange("m p n -> p m n"), ["s5w_%d" % q], [wk])
            for two in range(2):
                gp = (2 * q + two) % 8
                po = psf(0)[:, two * NJ:(two + 1) * NJ]
                for s_ in range(8):
                    mm(po, wide[:, gp, (7 - s_) * 16:(7 - s_) * 16 + 128], us5s[:, chn % 2, s_, :], s_ == 0, s_ == 7,
                       ["wide", "us5s_%d" % (chn % 2)], ["ps0"])
            tcopy("scalar", uf[q3].rearrange("p a b -> p (a b)"), psf(0)[:, 0:2 * NJ], ["ps0"], [ufk])
            for d in range(2):
                for part in range(2):
                    pi_ = 1 + part
                    pk = "ps%d" % pi_
                    po = psf(pi_)[:, 0:2 * NJ]
                    mm(po, wqi[:, d * 2 + part, :], uf[q3].rearrange("p a b -> p (a b)"), True, True, [wk, ufk], [pk])
                    xk = "XL%d%d%d" % (qi, d, part)
                    for two in range(2):
                        rows = slice(two * 64, (two + 1) * 64)
                        dst = XL[qi][d][part][rows, :]
                        if d == 1:
                            dst = rev(dst)
                        tcopy("scalar", dst, po[rows, two * NJ:(two + 1) * NJ], [pk], [xk])

        def tables(q):
            qi = q % 2
            DMA(twt[qi], tw_h.ap()[q], ["twtab_%d" % q], ["twt%d" % qi])

        def stageB(q):
            qi = q % 2
            for d in range(2):
                dq = d * 16 + q
                ta, tb_, tc, td = tmp4[d]
                ka, kb, kc_, kd = ["tmp%d%d" % (d, i) for i in range(4)]
                sk_ = ck_ = "twt%d" % qi
                xr_, xi_ = XL[qi][d][0], XL[qi][d][1]
                kr, ki = "XL%d%d0" % (qi, d), "XL%d%d1" % (qi, d)
                tt(ta, xr_, twc[qi][d], ALU.mult, [kr, ck_], [ka])
                ptt(tb_, xi_, tws[qi][d], ALU.mult, [ki, sk_], [kb])
                tt(XT[d][0], ta, tb_, ALU.add, [ka, kb], ["XT%d0" % d])
                tt(tc, xi_, twc[qi][d], ALU.mult, [ki, ck_], [kc_])
                ptt(td, xr_, tws[qi][d], ALU.mult, [kr, sk_], [kd])
                tt(XT[d][1], tc, td, ALU.subtract, [kc_, kd], ["XT%d1" % d])
                rb = rho_t[:, dq:dq + 1].to_broadcast([128, NJ])
                for part in range(2):
                    o_ = XL[qi][d][part]
                    i_ = XT[d][part]
                    V(lambda h, o_=o_, i_=i_, rb=rb: h.tensor_tensor_scan(out=o_, data0=rb, data1=i_, initial=0.0,
                                                                           op0=ALU.mult, op1=ALU.add),
                      ["rho_t", "XT%d%d" % (d, part)], ["XL%d%d%d" % (qi, d, part)])
                o_r, o_i = Xb[qi][d][0][:, :], Xb[qi][d][1][:, :]
                if d == 1:
                    o_r, o_i = rev(o_r), rev(o_i)
                tt(ta, xr_, twc[qi][d], ALU.mult, [kr, ck_], [ka])
                ptt(tb_, xi_, tws[qi][d], ALU.mult, [ki, sk_], [kb])
                tt(o_r, ta, tb_, ALU.subtract, [ka, kb], ["Xb%d%d0" % (qi, d)])
                tt(tc, xr_, tws[qi][d], ALU.mult, [kr, sk_], [kc_])
                ptt(td, xi_, twc[qi][d], ALU.mult, [ki, ck_], [kd])
                tt(o_i, tc, td, ALU.add, [kc_, kd], ["Xb%d%d1" % (qi, d)])

        def stageC(q):
            qi = q % 2
            q3 = q % 3
            wqi = wq[q3]
            wk = "wq%d" % q3
            ufk = "uf%d" % q3
            zi = (q // 4) % 2
            for two in range(2):
                gp = (2 * q + two) % 8
                rows = slice(two * 64, (two + 1) * 64)
                pi_ = 3 + two
                pk = "ps%d" % pi_
                po = psf(pi_)[:, 0:NJ]
                mm(po, wqi[:, 8 + two, :], uf[q3][:, two, :], True, False, [wk, ufk], [pk])
                for part in range(2):
                    mm(po[:, 1:NJ], wqi[rows, 4 + part, :], Xb[qi][0][part][rows, 0:NJ - 1], False, False,
                       [wk, "Xb%d0%d" % (qi, part)], [pk])
                for part in range(2):
                    mm(po[:, 0:NJ - 1], wqi[rows, 6 + part, :], Xb[qi][1][part][rows, 1:NJ], False, part == 1,
                       [wk, "Xb%d1%d" % (qi, part)], [pk])
                g_, s_ = gq[two], gs[two]
                gk, sk2 = "gq%d" % two, "gs%d" % two
                act(g_, po, AF.Square, [pk], [gk])
                act(g_, g_, AF.Identity, [gk, "one_c"], [gk], bias=one_c[:, 0:1], scale=0.044715)
                tt(g_, g_, po, ALU.mult, [gk, pk], [gk])
                act(s_, g_, AF.Sigmoid, [gk], [sk2], scale=1.5957691216057308)
                tt(zfold[zi][:, gp, :], s_, po, ALU.mult, [sk2, pk], ["zfold%d" % zi])
            if q % 4 == 3:
                chn = q // 4
                for s_i in range(8):
                    pi_ = 3 + s_i % 2
                    pk = "ps%d" % pi_
                    po = psf(pi_)[:, 0:NJ]
                    for gp in range(8):
                        mm(po, wide[:, s_i, (7 - gp) * 16:(7 - gp) * 16 + 128], zfold[zi][:, gp, :], gp == 0, gp == 7,
                           ["wide", "zfold%d" % zi], [pk])
                    tcopy("scalar", us5[:, chn, s_i:L:8], po, [pk], ["us5_%d" % chn])

        n_units = 12 * NTB
        unit_ctr = [0]

        def uhy_unit():
            k = unit_ctr[0]
            if k >= n_units:
                return
            unit_ctr[0] += 1
            mt = 4 + k // NTB
            tb = k % NTB
            wi = mt % 2
            wk = "wst2_%d" % wi
            if tb == 0:
                DMA(wst2[wi], win_v[:, :, mt * 128:(mt + 1) * 128], [], [wk], eng="gpsimd")
            pi_ = 5 + k % 3
            pk = "ps%d" % pi_
            po = psf(pi_)[:, 0:W]
            for kc in range(8):
                mm(po, wst2[wi][:, kc, :], hT[:, kc, tb * W:(tb + 1) * W], kc == 0, kc == 7, [wk, "hT"], [pk])
            tcopy("scalar", uhy[:, mt - 4, tb * W:(tb + 1) * W], po, [pk], ["uhy_%d" % (mt - 4)])

        per_q = -(-n_units // 16)
        tables(0)
        stageA(0)
        tables(1)
        stageA(1)
        stageB(0)
        for q in range(16):
            if (q + 2) % 4 == 0 and q + 2 < 16:
                s_major((q + 2) // 4)
            if q + 2 < 16:
                tables(q + 2)
                stageA(q + 2)
            for _ in range((per_q + 1) // 2):
                uhy_unit()
            if q + 1 < 16:
                stageB(q + 1)
            stageC(q)
            for _ in range(per_q // 2):
                uhy_unit()
        while unit_ctr[0] < n_units:
            uhy_unit()
        P.barrier()
        P.mark("b%d_p2_s5" % b)
        at(E0)
        gluw = AR.alloc([4, DS5], BF16)
        DMA(gluw, glu_w_h.ap().rearrange("(kc kl) n -> kl kc n", kl=128), [], ["gluw"], eng="gpsimd")
        gsg = AR.alloc([W], F32)
        for oc in range(4):
            for tb in range(NTB):
                pi_ = (oc * NTB + tb) % 2
                pk = "ps%d" % pi_
                po = psf(pi_)[:, 0:W]
                for kc in range(4):
                    mm(po, gluw[:, kc, oc * 128:(oc + 1) * 128], us5[:, kc, tb * W:(tb + 1) * W], kc == 0, kc == 3,
                       ["gluw", "us5_%d" % kc], [pk])
                act(gsg, po, AF.Sigmoid, [pk, "glu_b"], ["gsg"], bias=glu_b[:, oc:oc + 1])
                tt(s5o[:, oc, tb * W:(tb + 1) * W], gsg, us5[:, oc, tb * W:(tb + 1) * W], ALU.mult,
                   ["gsg", "us5_%d" % oc], ["s5o"])
        P.barrier()
        P.mark("b%d_p2_glu" % b)

        at(E0)
        vg = AR.alloc([12, L], BF16)
        at(Z0)
        scts = [AR.alloc([L], F32) for _ in range(2)]
        vv = vg[:, 0:4, :].rearrange("p c (r m) -> p c r m", r=2)
        for m in range(12):
            u_ = uhy[:, m, :]
            uk = "uhy_%d" % m
            sct = scts[m % 2]
            act(sct, u_, AF.Identity, [uk, "cw", "cb"], ["sct%d" % (m % 2)], bias=cb[:, m:m + 1], scale=cw[:, m, 1:2])
            stt(sct[:, 1:L], u_[:, 0:L - 1], cw[:, m, 0:1], sct[:, 1:L], ALU.mult, ALU.add, [uk, "cw", "sct%d" % (m % 2)], ["sct%d" % (m % 2)])
            if m >= 4:
                stt(vg[:, m, 0:L - 1], u_[:, 1:L], cw[:, m, 2:3], sct[:, 0:L - 1], ALU.mult, ALU.add,
                    [uk, "cw", "sct%d" % (m % 2)], ["vg_%d" % m])
                tcopy("vector", vg[:, m, L - 1:L], sct[:, L - 1:L], ["sct%d" % (m % 2)], ["vg_%d" % m])
            else:
                stt(vv[:, m, 0, :], u_[:, 1:L:2], cw[:, m, 2:3], sct[:, 0:L:2], ALU.mult, ALU.add,
                    [uk, "cw", "sct%d" % (m % 2)], ["vg_%d" % m])
                stt(vv[:, m, 1, 0:LH - 1], u_[:, 2:L:2], cw[:, m, 2:3], sct[:, 1:L - 1:2], ALU.mult, ALU.add,
                    [uk, "cw", "sct%d" % (m % 2)], ["vg_%d" % m])
                tcopy("vector", vv[:, m, 1, LH - 1:LH], sct[:, L - 1:L], ["sct%d" % (m % 2)], ["vg_%d" % m])
        P.barrier()
        P.mark("b%d_p3_sconv" % b)
        at(A0)
        zTq = AR.alloc([2, KH, 512], BF16)
        YY = AR.alloc([4, KH, 512], BF16)
        fq = [AR.alloc([4, KH, 128], BF16) for _ in range(2)]
        hq = [AR.alloc([4, 512], F32) for _ in range(2)]
        fwd_end = AR.top
        assert AR.top <= D0, (AR.top, D0)
        at(Z0)
        cA = [AR.alloc([512], F32) for _ in range(6)]
        cM = [AR.alloc([512], F32) for _ in range(8)]
        assert AR.top <= arena_bytes
        at(fwd_end)
        gq = [AR.alloc([2, KH, 128], BF16) for _ in range(2)]
        yT = [AR.alloc([512], F32) for _ in range(2)]
        assert AR.top <= D0, (AR.top, D0)
        for o in range(2):
            vkeys = ["vg_%d" % cc for cc in range(4)]
            for r_ in range(2):
                for mh in range(KH):
                    i_ = (r_ * KH + mh) % 2
                    pk = "ps%d" % i_
                    pv = psb(i_)[:, 0:512].rearrange("p (a b) -> p a b", a=4)
                    for cc in range(4):
                        tr(pv[:, cc, :], vv[:, cc, r_, mh * 128:(mh + 1) * 128], idb, ["vg_%d" % cc, "idb"], [pk])
                    tcopy("scalar" if i_ else "vector", zTq[:, r_, mh, :], psb(i_)[:, 0:512], [pk], ["zTq"])
            P.mark("b%d_p3_c%d_T" % (b, o))
            for kt in range(KH):
                bb_ = kt % 2
                fk, hk = "fq%d" % bb_, "hq%d" % bb_
                DMA(fq[bb_], Fq_h.ap()[kt], [], [fk])
                DMA(hq[bb_], hs_h.ap()[kt][:, :, o * 512:(o + 1) * 512], ["hspec_%d_%d" % (kt, o)], [hk])
                pb4 = 4 * bb_
                for base_, Xs in ((0, (0, 1)), (1, (1,)), (2, (2, 3)), (3, (3,))):
                    n_mm = len(Xs) * KH
                    i_mm = 0
                    for X in Xs:
                        for mh in range(KH):
                            mm(psf(pb4 + base_), fq[bb_][:, X, mh, :], zTq[:, X % 2, mh, :], i_mm == 0, i_mm == n_mm - 1,
                               [fk, "zTq"], ["ps%d" % (pb4 + base_)])
                            i_mm += 1
                kA, kAo, kB, kBo = ["ps%d" % (pb4 + X) for X in range(4)]
                AoS, BoS, A_, A2_, B_, B2_ = cA
                tcopy("scalar", AoS, psf(pb4 + 1), [kAo], ["cA0"])
                tcopy("scalar", BoS, psf(pb4 + 3), [kBo], ["cA1"])
                stt(A2_, AoS, -2.0, psf(pb4 + 0), ALU.mult, ALU.add, [kA, "cA0"], ["cA3"])
                stt(B2_, BoS, 2.0, psf(pb4 + 2), ALU.mult, ALU.subtract, [kB, "cA1"], ["cA5"])
                Ha, Hb, Ha2, Hb2 = [hq[bb_][:, i, :] for i in range(4)]
                tt(cM[0], psf(pb4 + 0), Ha, ALU.mult, [kA, hk], ["cM0"])
                tt(cM[1], psf(pb4 + 2), Hb, ALU.mult, [kB, hk], ["cM1"])
                tt(cM[2], psf(pb4 + 0), Hb, ALU.mult, [kA, hk], ["cM2"])
                tt(cM[3], psf(pb4 + 2), Ha, ALU.mult, [kB, hk], ["cM3"])
                ptt(cM[4], A2_, Ha2, ALU.mult, ["cA3", hk], ["cM4"])
                ptt(cM[5], B2_, Hb2, ALU.mult, ["cA5", hk], ["cM5"])
                ptt(cM[6], A2_, Hb2, ALU.mult, ["cA3", hk], ["cM6"])
                ptt(cM[7], B2_, Ha2, ALU.mult, ["cA5", hk], ["cM7"])
                tt(cM[0], cM[0], cM[1], ALU.subtract, ["cM0", "cM1"], ["cM0"])
                tt(cM[2], cM[2], cM[3], ALU.add, ["cM2", "cM3"], ["cM2"])
                ptt(cM[4], cM[4], cM[5], ALU.subtract, ["cM4", "cM5"], ["cM4"])
                ptt(cM[6], cM[6], cM[7], ALU.add, ["cM6", "cM7"], ["cM6"])
                tt(YY[:, 0, kt, :], cM[0], cM[4], ALU.add, ["cM0", "cM4"], ["YY"])
                tt(YY[:, 2, kt, :], cM[0], cM[4], ALU.subtract, ["cM0", "cM4"], ["YY"])
                tt(YY[:, 1, kt, :], cM[2], cM[6], ALU.subtract, ["cM2", "cM6"], ["YY"])
                tt(YY[:, 3, kt, :], cM[2], cM[6], ALU.add, ["cM2", "cM6"], ["YY"])
            P.mark("b%d_p3_c%d_fwd" % (b, o))
            for r_ in range(2):
                for mt in range(KH):
                    it = r_ * KH + mt
                    bb_ = it % 2
                    gk = "gq%d" % bb_
                    DMA(gq[bb_], Gq_h.ap()[r_, mt], [], [gk])
                    pi_ = 6 + bb_
                    pk = "ps%d" % pi_
                    for cs_ in range(2):
                        for kt in range(KH):
                            mm(psf(pi_), gq[bb_][:, cs_, kt, :], YY[:, 2 * r_ + cs_, kt, :], cs_ == 0 and kt == 0,
                               cs_ == 1 and kt == KH - 1, [gk, "YY"], [pk])
                    tcopy("scalar", yT[bb_], psf(pi_), [pk], ["yT%d" % bb_])
                    pj = bb_
                    pkj = "ps%d" % pj
                    pvj = psf(pj).rearrange("p (a b) -> p a b", a=4)
                    for cc in range(4):
                        tr(pvj[:, cc, :], yT[bb_][:, cc * 128:(cc + 1) * 128], idf, ["yT%d" % bb_, "idf"], [pkj])
                    t0_ = 2 * mt * 128 + r_
                    gate = vg[:, 4 + 4 * o:8 + 4 * o, t0_:t0_ + 255:2]
                    if o == 0:
                        dst = vv[:, :, r_, mt * 128:(mt + 1) * 128]
                    else:
                        dst = vg[:, 0:4, t0_:t0_ + 255:2]
                    tt(dst, pvj, gate, ALU.mult, [pkj] + ["vg_%d" % (4 + 4 * o + cc) for cc in range(4)], vkeys)
            if o == 1:
                P.barrier()
            P.mark("b%d_p3_c%d_inv" % (b, o))

        z2 = vg[:, 0:4, :]
        for tb in range(NTB):
            DMA(hT[:, :, tb * W:(tb + 1) * W], hts_h.ap()[:, :, tb * W:(tb + 1) * W], ["hts"], ["hT_%d" % tb])
        at(B0)
        wab = AR.alloc([4, D], BF16)
        wbb = AR.alloc([4, D], BF16)
        mg = AR.alloc([8, L], BF16)
        wg_ = [[AR.alloc([8, 128], BF16) for _ in range(2)] for _ in range(2)]
        for gi in range(2):
            DMA(wg_[0][gi], win_v[:, :, 2048 + gi * 1024:2048 + gi * 1024 + 128], [], ["wg0%d" % gi], eng="gpsimd")
        DMA(wab, wa_h.ap().rearrange("(kc kl) n -> kl kc n", kl=128), [], ["wab"], eng="gpsimd")
        DMA(wbb, wb_h.ap().rearrange("(kc kl) n -> kl kc n", kl=128), [], ["wbb"], eng="gpsimd")
        sg0 = AR.alloc([W], F32)
        sg1 = AR.alloc([W], F32)
        assert AR.top <= D0, (AR.top, D0)
        at(E0 + 4 * LR * 2)
        wo = AR.alloc([8, D], BF16)
        for dch in range(8):
            bb_ = dch % 2
            if dch == 2:
                DMA(wo, wout_h.ap().rearrange("(kc kl) n -> kl kc n", kl=128), [], ["wo"], eng="gpsimd")
            for gi in range(2):
                if dch == 0:
                    continue
                DMA(wg_[bb_][gi], win_v[:, :, 2048 + gi * 1024 + dch * 128:2048 + gi * 1024 + (dch + 1) * 128], [],
                    ["wg%d%d" % (bb_, gi)], eng="gpsimd")
            for tb in range(NTB):
                tsl = slice(tb * W, (tb + 1) * W)
                base = 4 * (tb % 2)
                pa, pb_, pg0, pg1 = base, base + 1, base + 2, base + 3
                for gi, pg in ((0, pg0), (1, pg1)):
                    for kc in range(8):
                        mm(psf(pg)[:, 0:W], wg_[bb_][gi][:, kc, :], hT[:, kc, tsl], kc == 0, kc == 7,
                           ["wg%d%d" % (bb_, gi), "hT_%d" % tb], ["ps%d" % pg])
                for kc in range(4):
                    mm(psf(pa)[:, 0:W], wab[:, kc, dch * 128:(dch + 1) * 128], s5o[:, kc, tsl], kc == 0, kc == 3,
                       ["wab", "s5o"], ["ps%d" % pa])
                for kc in range(4):
                    mm(psf(pb_)[:, 0:W], wbb[:, kc, dch * 128:(dch + 1) * 128], z2[:, kc, tsl], kc == 0, kc == 3,
                       ["wbb"] + ["vg_%d" % cc for cc in range(4)], ["ps%d" % pb_])
                act(sg0, psf(pg0)[:, 0:W], AF.Sigmoid, ["ps%d" % pg0], ["sg0"])
                act(sg1, psf(pg1)[:, 0:W], AF.Sigmoid, ["ps%d" % pg1], ["sg1"])
                tt(sg0, sg0, psf(pa)[:, 0:W], ALU.mult, ["sg0", "ps%d" % pa], ["sg0"])
                tt(sg1, sg1, psf(pb_)[:, 0:W], ALU.mult, ["sg1", "ps%d" % pb_], ["sg1"])
                tt(mg[:, dch, tsl], sg0, sg1, ALU.add, ["sg0", "sg1"], ["mg"])
        P.barrier()
        P.mark("b%d_p4_merge" % b)
        at(E0)
        rowg1 = AR.alloc([D], F32)
        rowA2 = AR.alloc([D], F32)
        rowB2 = AR.alloc([D], F32)
        load_row(rowg1, b, 2, "rowg1")
        load_row(rowB2, b, 3, "rowB2")
        load_row(rowA2, b, 4, "rowA2")
        rown = AR.alloc([D], F32)
        DMA(rown, bcast_rows(n2g_h, 0, D), [], ["rown"])
        stt(rowA2, rowA2, 1.0, rown, ALU.add, ALU.mult, ["rowA2", "rown"], ["rowA2"])
        assert AR.top <= E0 + 4 * LR * 2
        at(E0 + 4 * LR * 2 + 8 * D * 2)
        xt = [AR.alloc([D], F32) for _ in range(3)]
        x1 = [AR.alloc([D], F32) for _ in range(2)]
        sqs = [AR.alloc([D], F32) for _ in range(2)]
        sqm = [AR.alloc([D], F32) for _ in range(2)]
        hb_ = [AR.alloc([D], BF16) for _ in range(2)]
        ss = [AR.alloc([1], F32) for _ in range(2)]
        h2T = hT

        def p4_s0(t_):
            i3 = t_ % 3
            DMA(xt[i3], x_h.ap()[b, t_ * 128:(t_ + 1) * 128, :], [], ["xt%d" % i3])
            for hh in range(2):
                pi_ = 2 * i3 + hh
                for kc in range(8):
                    mm(psf(pi_), mg[:, kc, t_ * 128:(t_ + 1) * 128], wo[:, kc, hh * 512:(hh + 1) * 512], kc == 0, kc == 7,
                       ["mg", "wo"], ["ps%d" % pi_])

        def p4_s1(t_):
            i3 = t_ % 3
            i2 = t_ % 2
            xk = "xt%d" % i3
            x1k = "x1%d" % i2
            for hh in range(2):
                pi_ = 2 * i3 + hh
                cs = slice(hh * 512, (hh + 1) * 512)
                tt(x1[i2][:, cs], psf(pi_), rowg1[:, cs], ALU.mult, ["ps%d" % pi_, "rowg1"], [x1k])
            ptt(x1[i2], x1[i2], xt[i3], ALU.add, [x1k, xk], [x1k])
            DMA(x1s_h.ap()[t_ * 128:(t_ + 1) * 128, :], x1[i2], [x1k], ["x1s_%d" % t_])
            rms_rstd(x1[i2], x1k, sqs[i2], ss[i2], "b%d" % i2)
            stt(sqm[i2], x1[i2], ss[i2][:, 0:1], rowA2, ALU.mult, ALU.mult, [x1k, "ssb%d" % i2, "rowA2"], ["sqm%d" % i2])
            ptt(hb_[i2], sqm[i2], rowB2, ALU.add, ["sqm%d" % i2, "rowB2"], ["hb%d" % i2])

        def p4_s2(t_):
            i2 = t_ % 2
            pi_ = 6 + i2
            pk = "ps%d" % pi_
            pv = psb(pi_).rearrange("p (a b) -> p a b", a=8)
            for dc in range(8):
                tr(pv[:, dc, :], hb_[i2][:, dc * 128:(dc + 1) * 128], idb, ["hb%d" % i2, "idb"], [pk])
            tcopy("scalar", h2T[:, :, t_ * 128:(t_ + 1) * 128], pv, [pk], ["h2T"])

        p4_s0(0)
        if TT > 1:
            p4_s0(1)
        p4_s1(0)
        for t_ in range(TT):
            if t_ + 2 < TT:
                p4_s0(t_ + 2)
            if t_ + 1 < TT:
                p4_s1(t_ + 1)
            p4_s2(t_)
        P.barrier()
        P.mark("b%d_p4_wout" % b)

        at(B0)
        actT = AR.alloc([NF, L], BF16)
        at(B0 + NF * LR * 2)
        wd = AR.alloc([NF, D], BF16)
        wgu = [[AR.alloc([8, 256], BF16) for _ in range(2)] for _ in range(2)]
        sgl = AR.alloc([W], F32)
        rowg2 = AR.alloc([D], F32)
        rowfg = AR.alloc([D], F32)
        load_row(rowg2, b, 5, "rowg2")
        DMA(rowfg, bcast_rows(fg_h, 0, D), [], ["rowfg"])
        assert AR.top <= arena_bytes
        wgu_v = wgu_h.ap().rearrange("(kh kl) n -> kl kh n", kl=128)
        for f in range(NF):
            bb_ = (f // 2) % 2
            fo = (f % 2) * 128
            if f == 3:
                DMA(wd, wdn_h.ap().rearrange("(f p) n -> p f n", p=128), [], ["wd"], eng="gpsimd")
            if f % 2 == 0:
                DMA(wgu[bb_][0], wgu_v[:, :, f * 128:(f + 2) * 128], [], ["wgu%d0" % bb_], eng="gpsimd")
                DMA(wgu[bb_][1], wgu_v[:, :, DFF + f * 128:DFF + (f + 2) * 128], [], ["wgu%d1" % bb_], eng="gpsimd")
            for tb in range(NTB):
                tsl = slice(tb * W, (tb + 1) * W)
                base = 2 * ((f * NTB + tb) % 4)
                pg, pu = base, base + 1
                for kc in range(8):
                    mm(psf(pg)[:, 0:W], wgu[bb_][0][:, kc, fo:fo + 128], h2T[:, kc, tsl], kc == 0, kc == 7,
                       ["wgu%d0" % bb_, "h2T"], ["ps%d" % pg])
                for kc in range(8):
                    mm(psf(pu)[:, 0:W], wgu[bb_][1][:, kc, fo:fo + 128], h2T[:, kc, tsl], kc == 0, kc == 7,
                       ["wgu%d1" % bb_, "h2T"], ["ps%d" % pu])
                act(sgl, psf(pg)[:, 0:W], AF.Sigmoid, ["ps%d" % pg], ["sgl"])
                tt(sgl, sgl, psf(pg)[:, 0:W], ALU.mult, ["sgl", "ps%d" % pg], ["sgl"])
                tt(actT[:, f, tsl], sgl, psf(pu)[:, 0:W], ALU.mult, ["sgl", "ps%d" % pu], ["actT"])
        P.barrier()
        P.mark("b%d_p5_gu" % b)
        actT2 = actT
        at(A0)
        x1 = [AR.alloc([D], F32) for _ in range(2)]
        x2 = [AR.alloc([D], F32) for _ in range(2)]
        sqs = [AR.alloc([D], F32) for _ in range(2)]
        ss = [AR.alloc([1], F32) for _ in range(2)]
        assert AR.top <= B0, (AR.top, B0)
        for t_ in range(TT):
            i2 = t_ % 2
            x1k = "x1%d" % i2
            DMA(x1[i2], x1s_h.ap()[t_ * 128:(t_ + 1) * 128, :], ["x1s_%d" % t_], [x1k])
            p0, p1 = 2 * i2, 2 * i2 + 1
            for hh, pi_ in ((0, p0), (1, p1)):
                for f in range(NF):
                    mm(psf(pi_), actT2[:, f, t_ * 128:(t_ + 1) * 128], wd[:, f, hh * 512:(hh + 1) * 512], f == 0,
                       f == NF - 1, ["actT", "wd"], ["ps%d" % pi_])
            x2k = "x2%d" % i2
            for hh, pi_ in ((0, p0), (1, p1)):
                cs = slice(hh * 512, (hh + 1) * 512)
                tt(x2[i2][:, cs], psf(pi_), rowg2[:, cs], ALU.mult, ["ps%d" % pi_, "rowg2"], [x2k])
            ptt(x2[i2], x2[i2], x1[i2], ALU.add, [x2k, x1k], [x2k])
            rms_rstd(x2[i2], x2k, sqs[i2], ss[i2], "c%d" % i2)
            stt(x2[i2], x2[i2], ss[i2][:, 0:1], rowfg, ALU.mult, ALU.mult, [x2k, "ssc%d" % i2, "rowfg"], [x2k])
            DMA(out_h.ap()[b, t_ * 128:(t_ + 1) * 128, :], x2[i2], [x2k], ["out_%d_%d" % (b, t_)])
        P.barrier()
        P.mark("b%d_p5_down" % b)

    import os as _os
    if _os.environ.get("KMARKS"):
        import json as _json
        _json.dump(P.marks, open(_os.environ["KMARKS"], "w"))
    P.emit()
    st.close()
    return nc


def host_consts(L):
    N = 2 * L
    bf = ml_dtypes.bfloat16
    c = {}
    c["ident_f"] = np.eye(128, dtype=np.float32)
    c["ident_b"] = np.eye(128).astype(bf)
    wide = np.zeros((128, 8, 240), np.float32)
    for a in range(8):
        for ch in range(16):
            wide[a * 16 + ch, a, 112 + ch] = 1.0
    c["wide"] = wide.astype(bf)
    s_idx = np.arange(128) // 16
    c["mask_f"] = (s_idx[None, :] >= s_idx[:, None]).astype(np.float32)
    c["mask_b"] = (s_idx[:, None] >= s_idx[None, :]).astype(np.float32)
    t = np.arange(L, dtype=np.float64)
    t01 = t / max(L - 1, 1)
    bands = np.linspace(1e-4, 15.0, 16)
    ang = 2.0 * np.pi * t[:, None] * bands[None, :] / L
    feats = np.concatenate([t01[:, None], np.cos(ang), np.sin(ang)], axis=-1)
    c["featsT"] = np.ascontiguousarray(feats.T).astype(np.float32)
    c["negt01"] = np.ascontiguousarray((-t01).reshape(L // 128, 128).T).astype(np.float32)
    c["jidx"] = np.broadcast_to(np.arange(L // 8, dtype=np.float32)[None, :], (128, L // 8)).copy()
    LHn = L // 2
    KHn = LHn // 128
    k = np.arange(LHn, dtype=np.float64) + 0.5
    m = np.arange(LHn, dtype=np.float64)
    Fq = np.zeros((KHn, 128, 4, KHn, 128), np.float32)
    Gq = np.zeros((2, KHn, 128, 2, KHn, 128), np.float32)
    for r_ in range(2):
        th = 2.0 * np.pi * (2.0 * m + r_)[:, None] * k[None, :] / N
        for fi, fn in enumerate((np.cos, np.sin)):
            M = fn(th)
            X = 2 * fi + r_
            Fq[:, :, X, :, :] = M.reshape(KHn, 128, KHn, 128).transpose(2, 1, 0, 3)
            Gq[r_, :, :, fi, :, :] = (M.T * (2.0 / N)).reshape(KHn, 128, KHn, 128).transpose(2, 1, 0, 3)
    c["Fq"] = Fq.astype(bf)
    c["Gq"] = Gq.astype(bf)
    tq = np.zeros((128, 2, KHn), np.float32)
    for r_ in range(2):
        for mh in range(KHn):
            tq[:, r_, mh] = -t01[2 * (mh * 128 + np.arange(128)) + r_]
    c["negt01q"] = tq
    return c


def host_params(inp):
    f = np.float32
    p = {}
    for k_ in ["ada_w", "w_in", "s5_glu_w", "w_branch_a", "w_branch_b", "w_out", "ffn_w_gu", "ffn_w_down"]:
        p[k_] = np.ascontiguousarray(np.asarray(inp[k_], f)[0])
    p["ada_b"] = np.asarray(inp["ada_b"], f).reshape(1, -1)
    p["norm1_g"] = np.asarray(inp["norm1_g"], f).reshape(1, -1)
    p["norm2_g"] = np.asarray(inp["norm2_g"], f).reshape(1, -1)
    p["final_g"] = np.asarray(inp["final_g"], f).reshape(1, -1)

    def st_(a):
        return np.ascontiguousarray(np.asarray(a, f)[0].reshape(2, 16, 2, 64).transpose(2, 3, 0, 1).reshape(128, 32))

    p["lamre_t"] = st_(inp["s5_lam_re"])
    p["lamim_t"] = st_(inp["s5_lam_im"])
    ls = np.asarray(inp["s5_log_step"], f)[0].reshape(2, 16, 2).transpose(2, 0, 1)
    p["lstep_t"] = np.ascontiguousarray(np.broadcast_to(ls[:, None, :, :], (2, 64, 2, 16)).reshape(128, 32))

    def bt_(a):
        return np.ascontiguousarray(
            np.asarray(a, f)[0].reshape(2, 16, 2, 64, 16).transpose(2, 3, 0, 1, 4).reshape(128, 32, 16))

    def ct_(a):
        return np.ascontiguousarray(
            np.asarray(a, f)[0].reshape(2, 16, 2, 16, 64).transpose(2, 4, 0, 1, 3).reshape(128, 32, 16))

    p["bre_t"] = bt_(inp["s5_b_re"])
    p["bim_t"] = bt_(inp["s5_b_im"])
    p["cre_t"] = ct_(inp["s5_c_re"])
    p["cim_t"] = ct_(inp["s5_c_im"])
    dcol = np.asarray(inp["s5_d"], f)[0].reshape(32, 16).T
    p["dcol"] = np.ascontiguousarray(np.tile(dcol, (8, 1)))
    p["glu_b_col"] = np.ascontiguousarray(np.asarray(inp["s5_glu_b"], f)[0].reshape(4, 128).T)
    p["conv_w_col"] = np.ascontiguousarray(np.asarray(inp["hy_conv_w"], f)[0].reshape(3, 12, 128).transpose(2, 1, 0))
    p["conv_b_col"] = np.ascontiguousarray(np.asarray(inp["hy_conv_b"], f)[0].reshape(12, 128).T)
    p["hy_w1"] = np.ascontiguousarray(np.asarray(inp["hy_ffn_w1"], f)[0])
    p["hy_b1"] = np.asarray(inp["hy_ffn_b1"], f)[0].reshape(64, 1).copy()
    p["hy_w2"] = np.ascontiguousarray(np.asarray(inp["hy_ffn_w2"], f)[0])
    p["hy_b2"] = np.asarray(inp["hy_ffn_b2"], f)[0].reshape(64, 1).copy()
    p["hy_w3"] = np.ascontiguousarray(np.asarray(inp["hy_ffn_w3"], f)[0])
    p["hy_b3"] = np.asarray(inp["hy_ffn_b3"], f)[0].reshape(1, 2048).copy()
    p["hy_freq"] = np.asarray(inp["hy_freq"], f)[0].reshape(64, 1).copy()
    p["hy_decay"] = np.asarray(inp["hy_decay"], f)[0].reshape(1, 2048).copy()
    p["hy_bias"] = np.asarray(inp["hy_bias"], f)[0].reshape(1, 1024).copy()
    return p


def core_inputs(inp, params, consts, b0, NB):
    m = dict(params)
    m.update(consts)
    x = np.asarray(inp["x"], np.float32)
    c = np.asarray(inp["c"], np.float32)
    m["x"] = np.ascontiguousarray(x[b0:b0 + NB])
    m["cT"] = np.ascontiguousarray(c[b0:b0 + NB].reshape(NB, 8, 128).transpose(2, 1, 0))
    return m


_CACHE = {}


def kernel(**inputs):
    x = np.asarray(inputs["x"])
    B, L, _ = x.shape
    ncores = 8
    NB = B // ncores
    key = (L, NB)
    if key not in _CACHE:
        nc = bass.Bass("TRN2", target_bir_lowering=False)
        build(nc, L, NB)
        _CACHE[key] = (nc, host_consts(L))
    nc, consts = _CACHE[key]
    params = host_params(inputs)
    in_maps = [core_inputs(inputs, params, consts, i * NB, NB) for i in range(ncores)]
    res = run_bass_kernel_spmd(nc, in_maps, core_ids=list(range(ncores)))
    outs = [np.asarray(r["out"], np.float32) for r in res.results]
    return np.concatenate(outs, axis=0)
```

```python
import math
from contextlib import ExitStack

import numpy as np
import ml_dtypes

import concourse.bass as bass
import concourse.mybir as mybir
from concourse.bass_utils import run_bass_kernel_spmd

F32 = mybir.dt.float32
BF16 = mybir.dt.bfloat16
U8 = mybir.dt.uint8
AF = mybir.ActivationFunctionType
ALU = mybir.AluOpType
AX = mybir.AxisListType

ENGS = ["sync", "scalar", "vector", "gpsimd", "tensor"]
PI = math.pi
TWO_PI = 2.0 * math.pi

D = 1024
DS5 = 512
DHY = 512
DFF = 2816
NF = DFF // 128
EPS = 1e-6


class Prog:
    NDSEM = 12

    def __init__(self, nc):
        self.nc = nc
        self.ops = []
        self.lastw = {}
        self.readers = {}
        self.last_on = {e: None for e in ENGS}
        self.dma_rr = {e: 0 for e in ENGS}
        self.dma_last = {}

    def op(self, eng, fn, reads=(), writes=(), dma=False):
        deps = set()
        for r in reads:
            w = self.lastw.get(r)
            if w is not None:
                deps.add(w)
        for w_ in writes:
            w = self.lastw.get(w_)
            if w is not None:
                deps.add(w)
            for rd in self.readers.get(w_, {}).values():
                deps.add(rd)
        oid = len(self.ops)
        slot = None
        if dma:
            slot = self.dma_rr[eng] % self.NDSEM
            self.dma_rr[eng] += 1
            prev = self.dma_last.get((eng, slot))
            if prev is not None:
                deps.add(prev)
            self.dma_last[(eng, slot)] = oid
        self.ops.append(dict(eng=eng, fn=fn, deps=sorted(deps), dma=dma, slot=slot, marked=False))
        for r in reads:
            self.readers.setdefault(r, {})[eng if not dma else ("dma", oid)] = oid
        for w_ in writes:
            self.lastw[w_] = oid
            self.readers[w_] = {}
        self.last_on[eng] = oid
        return oid

    def mark(self, name):
        cnt = {e: 0 for e in ENGS}
        for o in self.ops:
            if o["fn"] is not None:
                cnt[o["eng"]] += 1
        self.marks = getattr(self, "marks", [])
        self.marks.append((name, cnt))

    def barrier(self):
        lasts = set()
        for e in ENGS:
            if self.last_on[e] is not None:
                lasts.add(self.last_on[e])
        for i, o in enumerate(self.ops):
            if o["dma"] and not o.get("barriered"):
                lasts.add(i)
                o["barriered"] = True
        for e in ENGS:
            oid = len(self.ops)
            self.ops.append(dict(eng=e, fn=None, deps=sorted(lasts), dma=False, slot=None, marked=False))
            self.last_on[e] = oid
        self.lastw = {}
        self.readers = {}

    def emit(self):
        nc = self.nc
        ops = self.ops
        for o in ops:
            for d in o["deps"]:
                do = ops[d]
                if do["eng"] == "tensor" and o["eng"] == "tensor" and not do["dma"] and not o["dma"]:
                    continue
                do["marked"] = True
        cnt = {e: 0 for e in ENGS}
        dcnt = {}
        for o in ops:
            if o["dma"]:
                k = (o["eng"], o["slot"])
                dcnt[k] = dcnt.get(k, 0) + 16
                o["tok"] = (("d",) + k, dcnt[k])
            elif o["marked"] and o["fn"] is not None:
                cnt[o["eng"]] += 1
                o["tok"] = (("c", o["eng"]), cnt[o["eng"]])
            else:
                o["tok"] = None
        with ExitStack() as st:
            sems = {}
            for e in ENGS:
                sems[("c", e)] = st.enter_context(nc.semaphore("c_" + e))
                for s in range(self.NDSEM):
                    if self.dma_rr[e] > s:
                        sems[("d", e, s)] = st.enter_context(nc.semaphore("d_%s_%d" % (e, s)))
            block = st.enter_context(nc.Block())
            per_eng = {e: [o for o in ops if o["eng"] == e] for e in ENGS}

            def run(e, h):
                seen = {}
                for o in per_eng[e]:
                    for d in o["deps"]:
                        do = ops[d]
                        if do["tok"] is None:
                            continue
                        if do["eng"] == "tensor" and e == "tensor" and not do["dma"] and not o["dma"]:
                            continue
                        key, val = do["tok"]
                        if seen.get(key, 0) >= val:
                            continue
                        seen[key] = val
                        h.wait_ge(sems[key], val)
                    if o["fn"] is None:
                        continue
                    ins = o["fn"](h)
                    if o["dma"]:
                        ins.then_inc(sems[o["tok"][0]], 16)
                    elif o["tok"] is not None:
                        ins.then_inc(sems[o["tok"][0]], 1)
                for (k, v) in dcnt.items():
                    if k[0] == e:
                        key = ("d",) + k
                        if seen.get(key, 0) < v:
                            h.wait_ge(sems[key], v)

            @block.sync
            def _(h):
                run("sync", h)

            @block.scalar
            def _(h):
                run("scalar", h)

            @block.vector
            def _(h):
                run("vector", h)

            @block.gpsimd
            def _(h):
                run("gpsimd", h)

            @block.tensor
            def _(h):
                run("tensor", h)


class Arena:
    def __init__(self, tensor, size):
        self.t = tensor
        self.size = size
        self.top = 0

    def alloc(self, free_shape, dtype):
        esz = 4 if dtype == F32 else 2
        n = esz
        for s in free_shape:
            n *= s
        off = (self.top + 63) // 64 * 64
        assert off + n <= self.size, ("SBUF arena overflow", off, n, self.size)
        self.top = off + n
        ap = self.t[:, off:off + n].bitcast(dtype)
        if len(free_shape) == 2:
            ap = ap.rearrange("p (a b) -> p a b", a=free_shape[0])
        elif len(free_shape) == 3:
            ap = ap.rearrange("p (a b c) -> p a b c", a=free_shape[0], b=free_shape[1])
        return ap

    def mark(self):
        return self.top

    def release(self, m):
        self.top = m


def build(nc, L, NB, arena_bytes=200 * 1024):
    TT = L // 128
    NJ = L // 8
    W = min(512, L)
    NTB = L // W
    KT = L // 128

    def din(name, shape, dt=F32):
        return nc.dram_tensor(name, list(shape), dt, kind="ExternalInput")

    x_h = din("x", [NB, L, D])
    cT_h = din("cT", [128, 8, NB])
    ada_w_h = din("ada_w", [D, 6 * D])
    ada_b_h = din("ada_b", [1, 6 * D])
    n1g_h = din("norm1_g", [1, D])
    n2g_h = din("norm2_g", [1, D])
    fg_h = din("final_g", [1, D])
    w_in_h = din("w_in", [D, 4096])
    lamre_h = din("lamre_t", [128, 32])
    lamim_h = din("lamim_t", [128, 32])
    lstep_h = din("lstep_t", [128, 32])
    bre_h = din("bre_t", [128, 32, 16])
    bim_h = din("bim_t", [128, 32, 16])
    cre_h = din("cre_t", [128, 32, 16])
    cim_h = din("cim_t", [128, 32, 16])
    dcol_h = din("dcol", [128, 32])
    glu_w_h = din("s5_glu_w", [DS5, DS5])
    glu_b_h = din("glu_b_col", [128, 4])
    cw_h = din("conv_w_col", [128, 12, 3])
    cb_h = din("conv_b_col", [128, 12])
    hw1_h = din("hy_w1", [33, 64])
    hb1_h = din("hy_b1", [64, 1])
    hw2_h = din("hy_w2", [64, 64])
    hb2_h = din("hy_b2", [64, 1])
    hw3_h = din("hy_w3", [64, 2048])
    hb3_h = din("hy_b3", [1, 2048])
    hfr_h = din("hy_freq", [64, 1])
    hdec_h = din("hy_decay", [1, 2048])
    hbias_h = din("hy_bias", [1, 1024])
    wa_h = din("w_branch_a", [DS5, D])
    wb_h = din("w_branch_b", [DHY, D])
    wout_h = din("w_out", [D, D])
    wgu_h = din("ffn_w_gu", [D, 2 * DFF])
    wdn_h = din("ffn_w_down", [DFF, D])
    idf_h = din("ident_f", [128, 128])
    idb_h = din("ident_b", [128, 128], BF16)
    wide_h = din("wide", [128, 8, 240], BF16)
    mkf_h = din("mask_f", [128, 128])
    mkb_h = din("mask_b", [128, 128])
    feats_h = din("featsT", [33, L])
    nt01_h = din("negt01", [128, TT])
    jidx_h = din("jidx", [128, NJ])
    KH = L // 256
    LH = L // 2
    Fq_h = din("Fq", [KH, 128, 4, KH, 128], BF16)
    Gq_h = din("Gq", [2, KH, 128, 2, KH, 128], BF16)
    nt01q_h = din("negt01q", [128, 2, KH])
    out_h = nc.dram_tensor("out", [NB, L, D], F32, kind="ExternalOutput")
    modrow_h = nc.dram_tensor("modrow", [NB, 6 * D], F32, kind="Internal")
    s5w_h = nc.dram_tensor("s5w", [16, 10, 128, 128], BF16, kind="Internal")
    hs_h = nc.dram_tensor("hspec", [KH, 128, 4, 1024], F32, kind="Internal")
    hts_h = nc.dram_tensor("hT_spill", [128, 8, L], BF16, kind="Internal")
    tw_h = nc.dram_tensor("twtab", [16, 128, 4, NJ], F32, kind="Internal")
    x1s_h = nc.dram_tensor("x1_spill", [L, D], F32, kind="Internal")

    st = ExitStack()
    arena_t = st.enter_context(nc.sbuf_tensor("arena", [128, arena_bytes], U8))
    AR = Arena(arena_t, arena_bytes)
    PS = [st.enter_context(nc.psum_tensor("ps%d" % i, [128, 512], F32)) for i in range(8)]
    P = Prog(nc)

    def psf(i):
        return PS[i][:]

    def psb(i):
        return PS[i][:].bitcast(BF16)

    def V(fn, r, w):
        P.op("vector", fn, r, w)

    def S(fn, r, w):
        P.op("scalar", fn, r, w)

    def T(fn, r, w):
        P.op("tensor", fn, r, w)

    def DMA(out, in_, r, w, eng="sync"):
        P.op(eng, lambda h: h.dma_start(out=out, in_=in_), r, w, dma=True)

    def bcast_rows(handle, offset, n):
        return bass.AP(handle, offset, [[0, 128], [1, n]])

    def tcopy(eng, out, in_, r, w):
        if eng == "vector":
            V(lambda h: h.tensor_copy(out=out, in_=in_), r, w)
        else:
            S(lambda h: h.activation(out=out, in_=in_, func=AF.Copy), r, w)

    def tt(out, a, b, op, r, w):
        V(lambda h: h.tensor_tensor(out=out, in0=a, in1=b, op=op), r, w)

    def ptt(out, a, b, op, r, w):
        P.op("vector", lambda h: h.tensor_tensor(out=out, in0=a, in1=b, op=op), r, w)

    def tsc(out, a, s1, s2, op0, op1, r, w):
        P.op("vector", lambda h: h.tensor_scalar(out=out, in0=a, scalar1=s1, scalar2=s2, op0=op0, op1=op1), r, w)

    MAGIC = 12582912.0
    INV2PI = 1.0 / TWO_PI

    def rr(out, x_, tmp, rk, wk, tk):
        tsc(tmp, x_, INV2PI, MAGIC, ALU.mult, ALU.add, rk, [tk])
        V(lambda h: h.tensor_single_scalar(out=tmp, in_=tmp, scalar=-MAGIC, op=ALU.add), [tk], [tk])
        stt(out, tmp, -TWO_PI, x_, ALU.mult, ALU.add, [tk] + list(rk), [wk])

    def stt(out, a, s, b, op0, op1, r, w):
        V(lambda h: h.scalar_tensor_tensor(out=out, in0=a, scalar=s, in1=b, op0=op0, op1=op1), r, w)

    def act(out, in_, func, r, w, bias=None, scale=None):
        kw = {}
        if bias is not None:
            kw["bias"] = bias
        if scale is not None:
            kw["scale"] = scale
        S(lambda h: h.activation(out=out, in_=in_, func=func, **kw), r, w)

    def mm(out, lhsT, rhs, start, stop, r, w):
        T(lambda h: h.matmul(out, lhsT, rhs, start=start, stop=stop), r, w)

    def tr(out, in_, ident, r, w):
        T(lambda h: h.transpose(out=out, in_=in_, identity=ident), r, w)

    idf = AR.alloc([128], F32)
    idb = AR.alloc([128], BF16)
    wide = AR.alloc([8, 240], BF16)
    ones_f = AR.alloc([128], F32)
    rho_t = AR.alloc([32], F32)
    psi_t = AR.alloc([32], F32)
    jidx = AR.alloc([NJ], F32)
    glu_b = AR.alloc([4], F32)
    cw = AR.alloc([12, 3], F32)
    cb = AR.alloc([12], F32)
    negpi = AR.alloc([1], F32)
    one_c = AR.alloc([1], F32)
    DMA(idf, idf_h.ap(), [], ["idf"])
    DMA(idb, idb_h.ap(), [], ["idb"])
    DMA(wide, wide_h.ap(), [], ["wide"])
    DMA(jidx, jidx_h.ap(), [], ["jidx"])
    DMA(glu_b, glu_b_h.ap(), [], ["glu_b"])
    DMA(cw, cw_h.ap(), [], ["cw"])
    DMA(cb, cb_h.ap(), [], ["cb"])
    V(lambda h: h.memset(ones_f, 1.0), [], ["ones_f"])
    V(lambda h: h.memset(one_c, 1.0), [], ["one_c"])
    V(lambda h: h.memset(negpi, 0.5 * PI), [], ["negpi"])
    P.barrier()
    P.mark("init")
    base_mark = (AR.mark() + 63) // 64 * 64
    LR = 2048
    A0 = base_mark
    B0 = A0 + 8 * LR * 2
    C0 = B0 + 4 * LR * 2
    D0 = C0 + 12 * LR * 2
    E0 = D0 + 4 * LR * 2
    Z0 = E0 + 12 * LR * 2

    def at(off):
        AR.top = off

    def rev(ap_):
        (ps_, pc_), (fs_, fc_) = ap_.ap
        return bass.AP(ap_.tensor, ap_.offset + (fc_ - 1) * fs_, [[ps_, pc_], [-fs_, fc_]])

    def gen_prepA():
        at(158 * 1024)
        cT = AR.alloc([8, NB], F32)
        scT = AR.alloc([8, NB], F32)
        adab = [AR.alloc([512], F32) for _ in range(2)]
        modsb = [AR.alloc([512], F32) for _ in range(2)]
        awb = [AR.alloc([8, 512], F32) for _ in range(2)]
        assert AR.top <= arena_bytes
        DMA(cT, cT_h.ap(), [], ["cT"])
        act(scT, cT, AF.Sigmoid, ["cT"], ["scT"])
        tt(scT, scT, cT, ALU.mult, ["scT", "cT"], ["scT"])
        adaw_v = ada_w_h.ap().rearrange("(kh kl) n -> kl kh n", kl=128)
        yield
        for blk in range(12):
            i2 = blk % 2
            wb_ = awb[i2]
            wk = "awb%d" % i2
            DMA(wb_, adaw_v[:, :, blk * 512:(blk + 1) * 512], [], [wk])
            DMA(adab[i2][0:1, :], ada_b_h.ap()[:, blk * 512:(blk + 1) * 512], [], ["adab%d" % i2])
            pk = "ps%d" % (6 + i2)
            po = psf(6 + i2)[0:NB, :]
            mm(po, ones_f[0:1, 0:NB], adab[i2][0:1, :], True, False, ["ones_f", "adab%d" % i2], [pk])
            for kh in range(8):
                mm(po, scT[:, kh, :], wb_[:, kh, :], False, kh == 7, ["scT", wk], [pk])
            tcopy("scalar", modsb[i2][0:NB, :], po, [pk], ["modsb%d" % i2])
            DMA(modrow_h.ap()[:, blk * 512:(blk + 1) * 512], modsb[i2][0:NB, :], ["modsb%d" % i2], ["modrow_%d" % blk])
            yield

    gA = gen_prepA()
    next(gA)
    at(A0)

    m0 = AR.mark()

    def sm(n=32):
        return AR.alloc([n], F32)

    lamre, lamim, lstep = sm(), sm(), sm()
    DMA(lamre, lamre_h.ap(), [], ["lamre"])
    DMA(lamim, lamim_h.ap(), [], ["lamim"])
    DMA(lstep, lstep_h.ap(), [], ["lstep"])
    bre = AR.alloc([32, 16], F32)
    bim = AR.alloc([32, 16], F32)
    cre = AR.alloc([32, 16], F32)
    cim = AR.alloc([32, 16], F32)
    dcol = sm()
    mkf = AR.alloc([128], F32)
    mkb = AR.alloc([128], F32)
    DMA(bre, bre_h.ap(), [], ["bre"])
    DMA(bim, bim_h.ap(), [], ["bim"])
    DMA(cre, cre_h.ap(), [], ["cre"])
    DMA(cim, cim_h.ap(), [], ["cim"])
    DMA(dcol, dcol_h.ap(), [], ["dcol"])
    DMA(mkf, mkf_h.ap(), [], ["mkf"])
    DMA(mkb, mkb_h.ap(), [], ["mkb"])

    _names = {}

    def key(ap_obj, name):
        _names[id(ap_obj)] = name
        return ap_obj

    def nm(a):
        return _names[id(a)]

    for a, n in [(lamre, "lamre"), (lamim, "lamim"), (lstep, "lstep")]:
        key(a, n)
    _tmpc = [0]

    def new(name=None, n=32):
        a = sm(n)
        _tmpc[0] += 1
        return key(a, name or ("t%d" % _tmpc[0]))

    def e_tt(o, a, b, op):
        tt(o, a, b, op, [nm(a), nm(b)], [nm(o)])

    def e_ts(o, a, s1, s2, op0, op1=None):
        if op1 is None:
            V(lambda h: h.tensor_single_scalar(out=o, in_=a, scalar=s1, op=op0), [nm(a)], [nm(o)])
        else:
            tsc(o, a, s1, s2, op0, op1, [nm(a)], [nm(o)])

    step = new("step")
    act(step, lstep, AF.Exp, ["lstep"], ["step"])
    aa = new("aa")
    e_tt(aa, lamre, step, ALU.mult)
    phi = new("phi")
    e_tt(phi, lamim, step, ALU.mult)

    def horner(o, xin, coefs):
        e_ts(o, xin, coefs[-1], coefs[-2], ALU.mult, ALU.add)
        for c in reversed(coefs[:-2]):
            e_tt(o, o, xin, ALU.mult)
            e_ts(o, o, float(c), None, ALU.add)

    mag = new("mag")
    horner(mag, aa, [1.0 / math.factorial(k) for k in range(13)])
    hr = new("hr")
    rrt = new("rrt")
    rr(hr, phi, rrt, ["phi"], "hr", "rrt")
    e_ts(hr, hr, 0.5, None, ALU.mult)
    hr2 = new("hr2")
    e_tt(hr2, hr, hr, ALU.mult)
    sh = new("sh")
    horner(sh, hr2, [(-1.0) ** k / math.factorial(2 * k + 1) for k in range(9)])
    e_tt(sh, sh, hr, ALU.mult)
    ch = new("ch")
    horner(ch, hr2, [(-1.0) ** k / math.factorial(2 * k) for k in range(9)])
    sinp, cosp = new("sinp"), new("cosp")
    e_tt(sinp, sh, ch, ALU.mult)
    e_ts(sinp, sinp, 2.0, None, ALU.mult)
    e_tt(cosp, sh, sh, ALU.mult)
    e_ts(cosp, cosp, -2.0, 1.0, ALU.mult, ALU.add)
    ar_, ai_ = new("ar"), new("ai")
    e_tt(ar_, mag, cosp, ALU.mult)
    e_tt(ai_, mag, sinp, ALU.mult)
    numr = new("numr")
    e_ts(numr, ar_, -1.0, None, ALU.add)
    den, t1, t2 = new("den"), new("t1"), new("t2")
    e_tt(den, lamre, lamre, ALU.mult)
    e_tt(t1, lamim, lamim, ALU.mult)
    e_tt(den, den, t1, ALU.add)
    V(lambda h: h.reciprocal(out=den, in_=den), ["den"], ["den"])
    cor, coi = new("cor"), new("coi")
    e_tt(cor, numr, lamre, ALU.mult)
    e_tt(t1, ai_, lamim, ALU.mult)
    e_tt(cor, cor, t1, ALU.add)
    e_tt(cor, cor, den, ALU.mult)
    e_tt(coi, ai_, lamre, ALU.mult)
    e_tt(t1, numr, lamim, ALU.mult)
    e_tt(coi, coi, t1, ALU.subtract)
    e_tt(coi, coi, den, ALU.mult)

    def cmul(orr, oi, a_r, a_i, b_r, b_i):
        e_tt(t1, a_r, b_r, ALU.mult)
        e_tt(t2, a_i, b_i, ALU.mult)
        e_tt(orr, t1, t2, ALU.subtract)
        e_tt(t1, a_r, b_i, ALU.mult)
        e_tt(t2, a_i, b_r, ALU.mult)
        e_tt(oi, t1, t2, ALU.add)

    pwr, pwi = [None] * 9, [None] * 9
    pwr[0], pwi[0] = new("pwr0"), new("pwi0")
    V(lambda h: h.memset(pwr[0], 1.0), [], ["pwr0"])
    V(lambda h: h.memset(pwi[0], 0.0), [], ["pwi0"])
    pwr[1], pwi[1] = ar_, ai_
    for k in range(2, 9):
        pwr[k], pwi[k] = new("pwr%d" % k), new("pwi%d" % k)
        cmul(pwr[k], pwi[k], pwr[k - 1], pwi[k - 1], ar_, ai_)
    ipr, ipi = [None] * 9, [None] * 9
    im2 = new("im2")
    e_tt(im2, mag, mag, ALU.mult)
    V(lambda h: h.reciprocal(out=im2, in_=im2), ["im2"], ["im2"])
    ipr[1], ipi[1] = new("ipr1"), new("ipi1")
    e_tt(ipr[1], ar_, im2, ALU.mult)
    e_tt(ipi[1], ai_, im2, ALU.mult)
    e_ts(ipi[1], ipi[1], -1.0, None, ALU.mult)
    for k in range(2, 9):
        ipr[k], ipi[k] = new("ipr%d" % k), new("ipi%d" % k)
        cmul(ipr[k], ipi[k], ipr[k - 1], ipi[k - 1], ipr[1], ipi[1])
    e_tt(t1, mag, mag, ALU.mult)
    e_tt(t1, t1, t1, ALU.mult)
    tt(rho_t, t1, t1, ALU.mult, ["t1"], ["rho_t"])
    e_ts(t2, phi, 8.0, None, ALU.mult)
    rr(psi_t, t2, rrt, ["t2"], "psi_t", "rrt")

    def b3(a):
        return a.unsqueeze(2).to_broadcast([128, 32, 16])

    def b3h(a, d):
        return a[:, d * 16:(d + 1) * 16].unsqueeze(2).to_broadcast([128, 16, 16])

    bbr = AR.alloc([32, 16], F32)
    bbi = AR.alloc([32, 16], F32)
    w1_ = AR.alloc([32, 16], F32)
    w2_ = AR.alloc([32, 16], F32)
    tt(w1_, bre, b3(cor), ALU.mult, ["bre", "cor"], ["w1_"])
    tt(w2_, bim, b3(coi), ALU.mult, ["bim", "coi"], ["w2_"])
    tt(bbr, w1_, w2_, ALU.subtract, ["w1_", "w2_"], ["bbr"])
    tt(w1_, bim, b3(cor), ALU.mult, ["bim", "cor"], ["w1_"])
    tt(w2_, bre, b3(coi), ALU.mult, ["bre", "coi"], ["w2_"])
    tt(bbi, w1_, w2_, ALU.add, ["w1_", "w2_"], ["bbi"])

    Bt = [AR.alloc([32, 8, 16], F32) for _ in range(2)]
    Bti = [AR.alloc([32, 8, 16], F32) for _ in range(2)]
    Cs = [AR.alloc([32, 8, 16], F32) for _ in range(2)]

    def cprod(dst_r, dst_i, sr, si, xr_, xi_, d, s, neg_im=False, names=("x", "y")):
        sl = slice(d * 16, (d + 1) * 16)
        o_r = dst_r[:, sl, s, :]
        o_i = dst_i[:, sl, s, :]
        a1 = w1_[:, sl, :]
        a2 = w2_[:, sl, :]
        rk = [nm(sr), nm(si), names[0], names[1]]
        tt(a1, xr_[:, sl, :], b3h(sr, d), ALU.mult, rk, ["w1_"])
        tt(a2, xi_[:, sl, :], b3h(si, d), ALU.mult, rk, ["w2_"])
        tt(o_r, a1, a2, ALU.subtract, ["w1_", "w2_"], [names[2]])
        tt(a1, xi_[:, sl, :], b3h(sr, d), ALU.mult, rk, ["w1_"])
        tt(a2, xr_[:, sl, :], b3h(si, d), ALU.mult, rk, ["w2_"])
        if neg_im:
            tt(o_i, a1, a2, ALU.add, ["w1_", "w2_"], [names[3]])
            V(lambda h: h.tensor_single_scalar(out=o_i, in_=o_i, scalar=-1.0, op=ALU.mult), [names[3]], [names[3]])
        else:
            tt(o_i, a1, a2, ALU.add, ["w1_", "w2_"], [names[3]])

    for d in range(2):
        for s in range(8):
            next(gA, None)
            e = 7 - s if d == 0 else s
            cprod(Bt[0], Bt[1], pwr[e], pwi[e], bbr, bbi, d, s, names=("bbr", "bbi", "Bt0", "Bt1"))
            k = s + 1 if d == 0 else 8 - s
            cprod(Bti[0], Bti[1], ipr[k], ipi[k], bbr, bbi, d, s, names=("bbr", "bbi", "Bti0", "Bti1"))
            f = s + 1 if d == 0 else 8 - s
            cprod(Cs[0], Cs[1], pwr[f], pwi[f], cre, cim, d, s, neg_im=True, names=("cre", "cim", "Cs0", "Cs1"))

    stage = [AR.alloc([10, 128], BF16) for _ in range(2)]
    gtmp = AR.alloc([128], F32)
    gtmp2 = AR.alloc([128], F32)
    for q in range(16):
        next(gA, None)
        sg = stage[q % 2]
        sk = "stage%d" % (q % 2)
        for d in range(2):
            dq = d * 16 + q
            for part in range(2):
                pk = "ps%d" % ((d * 2 + part) % 4)
                po = psf((d * 2 + part) % 4)[:, 0:128]
                src = Bt[part][:, dq, :, :].rearrange("p s c -> p (s c)")
                tr(po, src, idf, ["Bt%d" % part, "idf"], [pk])
                tcopy("scalar", sg[:, d * 2 + part, :], po, [pk], [sk])
                csrc = Cs[part][:, dq, :, :].rearrange("p s c -> p (s c)")
                tcopy("vector", sg[:, 4 + d * 2 + part, :], csrc, ["Cs%d" % part], [sk])
        for two in range(2):
            g = 2 * q + two
            rows = slice(two * 64, (two + 1) * 64)
            for d in range(2):
                dq = d * 16 + q
                pk = "ps%d" % (4 + d)
                po = psf(4 + d)[:, 0:128]
                l0 = Bti[0][rows, dq, :, :].rearrange("p s c -> p (s c)")
                l1 = Bti[1][rows, dq, :, :].rearrange("p s c -> p (s c)")
                r0 = Cs[0][rows, dq, :, :].rearrange("p s c -> p (s c)")
                r1 = Cs[1][rows, dq, :, :].rearrange("p s c -> p (s c)")
                mm(po, l0, r0, True, False, ["Bti0", "Cs0"], [pk])
                mm(po, l1, r1, False, True, ["Bti1", "Cs1"], [pk])
            tt(gtmp, psf(4)[:, 0:128], mkf, ALU.mult, ["ps4", "mkf"], ["gtmp"])
            tt(gtmp2, psf(5)[:, 0:128], mkb, ALU.mult, ["ps5", "mkb"], ["gtmp2"])
            tt(gtmp, gtmp, gtmp2, ALU.add, ["gtmp", "gtmp2"], ["gtmp"])
            stt(sg[:, 8 + two, :], idf, dcol[:, g:g + 1], gtmp, ALU.mult, ALU.add, ["idf", "dcol", "gtmp"], [sk])
        DMA(s5w_h.ap()[q].rearrange("m p n -> p m n"), sg, [sk], ["s5w_%d" % q])
    for _ in gA:
        pass
    assert AR.top <= 158 * 1024, AR.top
    P.barrier()
    P.mark("prepB_s5")
    at(A0)

    m0 = AR.mark()
    hw1 = AR.alloc([64], F32)
    hw2 = AR.alloc([64], F32)
    hw3 = AR.alloc([2048], F32)
    hb1 = AR.alloc([1], F32)
    hb2 = AR.alloc([1], F32)
    hfr = AR.alloc([1], F32)
    hb3 = AR.alloc([2048], F32)
    adec = AR.alloc([2048], F32)
    hbias = AR.alloc([1024], F32)
    nt01 = AR.alloc([2, KH], F32)
    feats = AR.alloc([L], F32)
    DMA(hw1[0:33, :], hw1_h.ap(), [], ["hw1"])
    DMA(hw2[0:64, :], hw2_h.ap(), [], ["hw2"])
    DMA(hw3[0:64, :], hw3_h.ap(), [], ["hw3"])
    DMA(hb1[0:64, :], hb1_h.ap(), [], ["hb1"])
    DMA(hb2[0:64, :], hb2_h.ap(), [], ["hb2"])
    DMA(hfr[0:64, :], hfr_h.ap(), [], ["hfr"])
    DMA(hw3[64:65, :], hb3_h.ap(), [], ["hw3"])
    DMA(adec, bcast_rows(hdec_h, 0, 2048), [], ["adec"])
    DMA(hbias, bcast_rows(hbias_h, 0, 1024), [], ["hbias"])
    DMA(nt01, nt01q_h.ap(), [], ["nt01"])
    DMA(feats[0:33, :], feats_h.ap(), [], ["feats"])
    act(adec, adec, AF.Abs, ["adec"], ["adec"])
    tg_a = AR.alloc([NJ], F32)
    tg_m = AR.alloc([NJ], F32)
    tg_r = AR.alloc([NJ], F32)
    tg_s = [AR.alloc([4, NJ], F32) for _ in range(2)]
    for q in range(16):
        sg_ = tg_s[q % 2]
        sgk = "tg_s%d" % (q % 2)
        for d in range(2):
            dq = d * 16 + q
            V(lambda h, dq=dq: h.tensor_scalar_mul(out=tg_a, in0=jidx, scalar1=psi_t[:, dq:dq + 1]),
              ["jidx", "psi_t"], ["tg_a"])
            rr(tg_r, tg_a, tg_m, ["tg_a"], "tg_r", "tg_m")
            act(sg_[:, 2 * d, :], tg_r, AF.Sin, ["tg_r"], [sgk], scale=0.999998)
            act(tg_r, tg_r, AF.Abs, ["tg_r"], ["tg_r"])
            act(sg_[:, 2 * d + 1, :], tg_r, AF.Sin, ["tg_r", "negpi"], [sgk], bias=negpi[:, 0:1], scale=-1.0)
        DMA(tw_h.ap()[q], sg_, [sgk], ["twtab_%d" % q])
    h1 = AR.alloc([L], F32)
    h2 = AR.alloc([L], F32)
    argt = AR.alloc([W], F32)
    argr = AR.alloc([W], F32)
    argm = AR.alloc([W], F32)
    for (src, wmat, bcol, dst, sk_, wk_, bk_, dk_, kk) in [
        (feats, hw1, hb1, h1, "feats", "hw1", "hb1", "h1", 33),
        (h1, hw2, hb2, h2, "h1", "hw2", "hb2", "h2", 64),
    ]:
        for tb in range(NTB):
            po = psf(tb % 2)[0:64, 0:W]
            pk = "ps%d" % (tb % 2)
            mm(po, wmat[0:kk, 0:64], src[0:kk, tb * W:(tb + 1) * W], True, True, [wk_, sk_], [pk])
            a_ = argt[0:64, :]
            tsc(a_, po, bcol[0:64, 0:1], hfr[0:64, 0:1], ALU.add, ALU.mult, [pk, bk_, "hfr"], ["argt"])
            rr(argr[0:64, :], a_, argm[0:64, :], ["argt"], "argr", "argm")
            act(dst[0:64, tb * W:(tb + 1) * W], argr[0:64, :], AF.Sin, ["argr"], [dk_], scale=0.999998)
    V(lambda h: h.memset(h2[64:65, :], 1.0), [], ["h2"])
    ET = AR.alloc([2, KH, 1024], BF16)
    OT = AR.alloc([2, KH, 1024], BF16)
    hfw = AR.alloc([512], F32)
    hbw = AR.alloc([512], F32)
    dect = AR.alloc([512], F32)
    for r_ in range(2):
        for mh in range(KH):
            c0 = 2 * mh * 128 + r_
            h2s = h2[0:65, c0:c0 + 255:2]
            for o in range(2):
                for dr in range(2):
                    cbk = o * 2 + dr
                    cols = slice(cbk * 512, (cbk + 1) * 512)
                    pi_ = 2 + dr
                    pk = "ps%d" % pi_
                    po = psf(pi_)
                    mm(po, h2s, hw3[0:65, cols], True, True, ["h2", "hw3"], [pk])
                    act(dect, adec[:, cols], AF.Exp, ["adec", "nt01"], ["dect"], scale=nt01[:, r_, mh:mh + 1])
                    dst = hfw if dr == 0 else hbw
                    tt(dst, po, dect, ALU.mult, [pk, "dect"], ["hfw" if dr == 0 else "hbw"])
                if r_ == 0 and mh == 0:
                    V(lambda h: h.memset(hbw[0:1, :], 0.0), ["hbw"], ["hbw"])
                tt(ET[:, r_, mh, o * 512:(o + 1) * 512], hfw, hbw, ALU.add, ["hfw", "hbw"], ["ET"])
                tt(OT[:, r_, mh, o * 512:(o + 1) * 512], hfw, hbw, ALU.subtract, ["hfw", "hbw"], ["OT"])
    fblk = [AR.alloc([4, KH, 128], BF16) for _ in range(2)]
    hst = [AR.alloc([4, 512], F32) for _ in range(2)]
    tA = [AR.alloc([512], F32) for _ in range(2)]
    tB = [AR.alloc([512], F32) for _ in range(2)]
    tC = [AR.alloc([512], F32) for _ in range(2)]
    for kt in range(KH):
        bb_ = kt % 2
        fk = "fblk%d" % bb_
        DMA(fblk[bb_], Fq_h.ap()[kt], [], [fk])
        for hh in range(2):
            u_ = (kt * 2 + hh) % 2
            cs = slice(hh * 512, (hh + 1) * 512)
            for X in range(4):
                srcT = ET if X < 2 else OT
                r_ = X % 2
                pi_ = 4 * u_ + X
                for mh in range(KH):
                    mm(psf(pi_), fblk[bb_][:, X, mh, :], srcT[:, r_, mh, cs], mh == 0, mh == KH - 1,
                       [fk, "ET" if X < 2 else "OT"], ["ps%d" % pi_])
            pe, po_, pbe, pbo = [4 * u_ + X for X in range(4)]
            hk = "hst%d" % u_
            tcopy("scalar", tA[u_], psf(po_), ["ps%d" % po_], ["tA%d" % u_])
            tt(tB[u_], psf(pe), hbias[:, cs], ALU.add, ["ps%d" % pe, "hbias"], ["tB%d" % u_])
            tt(hst[u_][:, 0, :], tB[u_], tA[u_], ALU.add, ["tA%d" % u_, "tB%d" % u_], [hk])
            tt(hst[u_][:, 2, :], tB[u_], tA[u_], ALU.subtract, ["tA%d" % u_, "tB%d" % u_], [hk])
            tcopy("scalar", tC[u_], psf(pbo), ["ps%d" % pbo], ["tC%d" % u_])
            tt(hst[u_][:, 1, :], psf(pbe), tC[u_], ALU.add, ["ps%d" % pbe, "tC%d" % u_], [hk])
            tt(hst[u_][:, 3, :], tC[u_], psf(pbe), ALU.subtract, ["ps%d" % pbe, "tC%d" % u_], [hk])
            DMA(hs_h.ap()[kt][:, :, cs], hst[u_], [hk], ["hspec_%d_%d" % (kt, hh)], eng="scalar" if u_ else "sync")
    P.barrier()
    P.mark("prepC_hy")
    AR.release(m0)

    win_v = w_in_h.ap().rearrange("(kh kl) n -> kl kh n", kl=128)

    def load_row(dst, b, idx, kname):
        DMA(dst, bcast_rows(modrow_h, b * 6 * D + idx * D, D), ["modrow_%d" % k_ for k_ in range(12)], [kname])

    def rms_rstd(xt, xk, sq, ss, tag):
        act(sq, xt, AF.Square, [xk], ["sq" + tag])
        V(lambda h: h.reduce_sum(out=ss, in_=sq, axis=AX.X), ["sq" + tag], ["ss" + tag])
        tsc(ss, ss, 1.0 / D, EPS, ALU.mult, ALU.add, ["ss" + tag], ["ss" + tag])
        act(ss, ss, AF.Sqrt, ["ss" + tag], ["ss" + tag])
        V(lambda h: h.reciprocal(out=ss, in_=ss), ["ss" + tag], ["ss" + tag])

    for b in range(NB):
        at(A0)
        hT = AR.alloc([8, L], BF16)
        at(B0)
        us5 = AR.alloc([4, L], BF16)
        at(C0)
        uhy = AR.alloc([12, L], BF16)
        at(D0)
        rowA = AR.alloc([D], F32)
        rowB = AR.alloc([D], F32)
        rowg = AR.alloc([D], F32)
        load_row(rowB, b, 0, "rowB")
        load_row(rowA, b, 1, "rowA")
        DMA(rowg, bcast_rows(n1g_h, 0, D), [], ["rowg"])
        stt(rowA, rowA, 1.0, rowg, ALU.add, ALU.mult, ["rowA", "rowg"], ["rowA"])
        xt = [AR.alloc([D], F32) for _ in range(3)]
        sqs = [AR.alloc([D], F32) for _ in range(3)]
        sqm = [AR.alloc([D], F32) for _ in range(2)]
        hb_ = [AR.alloc([D], BF16) for _ in range(2)]
        ss = [AR.alloc([1], F32) for _ in range(3)]

        def p1_s0(t_):
            i3 = t_ % 3
            xk = "xt%d" % i3
            DMA(xt[i3], x_h.ap()[b, t_ * 128:(t_ + 1) * 128, :], [], [xk])
            rms_rstd(xt[i3], xk, sqs[i3], ss[i3], "a%d" % i3)

        def p1_s1(t_):
            i2 = t_ % 2
            i3 = t_ % 3
            xk = "xt%d" % i3
            stt(sqm[i2], xt[i3], ss[i3][:, 0:1], rowA, ALU.mult, ALU.mult, [xk, "ssa%d" % i3, "rowA"], ["sqm%d" % i2])
            ptt(hb_[i2], sqm[i2], rowB, ALU.add, ["sqm%d" % i2, "rowB"], ["hb%d" % i2])

        def p1_s2(t_):
            i2 = t_ % 2
            pk = "ps%d" % i2
            pv = psb(i2).rearrange("p (a b) -> p a b", a=8)
            for dc in range(8):
                tr(pv[:, dc, :], hb_[i2][:, dc * 128:(dc + 1) * 128], idb, ["hb%d" % i2, "idb"], [pk])
            tcopy("scalar", hT[:, :, t_ * 128:(t_ + 1) * 128], pv, [pk], ["hT"])

        p1_s0(0)
        if TT > 1:
            p1_s0(1)
        p1_s1(0)
        for t_ in range(TT):
            if t_ + 2 < TT:
                p1_s0(t_ + 2)
            if t_ + 1 < TT:
                p1_s1(t_ + 1)
            p1_s2(t_)
        DMA(hts_h.ap(), hT, ["hT"], ["hts"])
        P.mark("b%d_p1_norm" % b)
        wst = [AR.alloc([8, 512], BF16) for _ in range(2)]
        for mt in range(4):
            wi = (mt // 4) % 2
            wk = "wst%d" % wi
            if mt % 4 == 0:
                DMA(wst[wi], win_v[:, :, mt * 128:(mt + 4) * 128], [], [wk], eng="gpsimd")
            for tb in range(NTB):
                pi_ = 2 + (mt * NTB + tb) % 4
                pk = "ps%d" % pi_
                po = psf(pi_)[:, 0:W]
                for kc in range(8):
                    mm(po, wst[wi][:, kc, (mt % 4) * 128:(mt % 4 + 1) * 128], hT[:, kc, tb * W:(tb + 1) * W], kc == 0, kc == 7,
                       [wk, "hT"], [pk])
                if mt < 4:
                    dst, dk = us5[:, mt, tb * W:(tb + 1) * W], "us5_%d" % mt
                else:
                    dst, dk = uhy[:, mt - 4, tb * W:(tb + 1) * W], "uhy_%d" % (mt - 4)
                tcopy("vector" if (mt + tb) % 2 else "scalar", dst, po, [pk], [dk])
        P.barrier()
        P.mark("b%d_p1_win" % b)

        at(D0)
        s5o = AR.alloc([4, L], BF16)
        at(E0)
        wq = [AR.alloc([10, 128], BF16) for _ in range(3)]
        uf = [AR.alloc([2, NJ], BF16) for _ in range(3)]
        XL = [[[AR.alloc([NJ], F32) for _ in range(2)] for _ in range(2)] for _ in range(2)]
        Xb = [[[AR.alloc([NJ], BF16) for _ in range(2)] for _ in range(2)] for _ in range(2)]
        XT = [[AR.alloc([NJ], F32) for _ in range(2)] for _ in range(2)]
        twt = [AR.alloc([4, NJ], F32) for _ in range(2)]
        tws = [[twt[i][:, 2 * d, :] for d in range(2)] for i in range(2)]
        twc = [[twt[i][:, 2 * d + 1, :] for d in range(2)] for i in range(2)]
        wst2 = [AR.alloc([8, 128], BF16) for _ in range(2)]
        gluw = AR.alloc([4, DS5], BF16)
        gsg = AR.alloc([W], F32)
        DMA(gluw, glu_w_h.ap().rearrange("(kc kl) n -> kl kc n", kl=128), [], ["gluw"], eng="gpsimd")
        tmp4 = [[AR.alloc([NJ], F32) for _ in range(4)] for _ in range(2)]
        zfold = [AR.alloc([8, NJ], BF16) for _ in range(2)]
        gq = [AR.alloc([NJ], F32) for _ in range(2)]
        gs = [AR.alloc([NJ], F32) for _ in range(2)]
        us5s = AR.alloc([2, 8, NJ], BF16)

        def s_major(chn):
            P.op("gpsimd", lambda h, chn=chn, us5s=us5s, us5=us5: h.tensor_copy(
                out=us5s[:, chn % 2, :, :], in_=us5[:, chn, :].rearrange("p (j s) -> p s j", s=8)),
                ["us5_%d" % chn], ["us5s_%d" % (chn % 2)])

        s_major(0)
        s5w_v = s5w_h.ap()
        SINS = 0.999998

        def GP(fn, r, w):
            P.op("gpsimd", fn, r, w)

        def stageA(q):
            qi = q % 2
            q3 = q % 3
            wqi = wq[q3]
            wk = "wq%d" % q3
            chn = q // 4
            ufk = "uf%d" % q3
            DMA(wqi, s5w_v[q].rearrange("m p n -> p m n"), ["s5w_%d" % q], [wk])
            for two in range(2):
                gp = (2 * q + two) % 8
                po = psf(0)[:, two * NJ:(two + 1) * NJ]
                for s_ in range(8):
                    mm(po, wide[:, gp, (7 - s_) * 16:(7 - s_) * 16 + 128], us5s[:, chn % 2, s_, :], s_ == 0, s_ == 7,
                       ["wide", "us5s_%d" % (chn % 2)], ["ps0"])
            tcopy("scalar", uf[q3].rearrange("p a b -> p (a b)"), psf(0)[:, 0:2 * NJ], ["ps0"], [ufk])
            for d in range(2):
                for part in range(2):
                    pi_ = 1 + part
                    pk = "ps%d" % pi_
                    po = psf(pi_)[:, 0:2 * NJ]
                    mm(po, wqi[:, d * 2 + part, :], uf[q3].rearrange("p a b -> p (a b)"), True, True, [wk, ufk], [pk])
                    xk = "XL%d%d%d" % (qi, d, part)
                    for two in range(2):
                        rows = slice(two * 64, (two + 1) * 64)
                        dst = XL[qi][d][part][rows, :]
                        if d == 1:
                            dst = rev(dst)
                        tcopy("scalar", dst, po[rows, two * NJ:(two + 1) * NJ], [pk], [xk])

        def tables(q):
            qi = q % 2
            DMA(twt[qi], tw_h.ap()[q], ["twtab_%d" % q], ["twt%d" % qi])

        def stageB(q):
            qi = q % 2
            for d in range(2):
                dq = d * 16 + q
                ta, tb_, tc, td = tmp4[d]
                ka, kb, kc_, kd = ["tmp%d%d" % (d, i) for i in range(4)]
                sk_ = ck_ = "twt%d" % qi
                xr_, xi_ = XL[qi][d][0], XL[qi][d][1]
                kr, ki = "XL%d%d0" % (qi, d), "XL%d%d1" % (qi, d)
                tt(ta, xr_, twc[qi][d], ALU.mult, [kr, ck_], [ka])
                ptt(tb_, xi_, tws[qi][d], ALU.mult, [ki, sk_], [kb])
                tt(XT[d][0], ta, tb_, ALU.add, [ka, kb], ["XT%d0" % d])
                tt(tc, xi_, twc[qi][d], ALU.mult, [ki, ck_], [kc_])
                ptt(td, xr_, tws[qi][d], ALU.mult, [kr, sk_], [kd])
                tt(XT[d][1], tc, td, ALU.subtract, [kc_, kd], ["XT%d1" % d])
                rb = rho_t[:, dq:dq + 1].to_broadcast([128, NJ])
                for part in range(2):
                    o_ = XL[qi][d][part]
                    i_ = XT[d][part]
                    V(lambda h, o_=o_, i_=i_, rb=rb: h.tensor_tensor_scan(out=o_, data0=rb, data1=i_, initial=0.0,
                                                                           op0=ALU.mult, op1=ALU.add),
                      ["rho_t", "XT%d%d" % (d, part)], ["XL%d%d%d" % (qi, d, part)])
                o_r, o_i = Xb[qi][d][0][:, :], Xb[qi][d][1][:, :]
                if d == 1:
                    o_r, o_i = rev(o_r), rev(o_i)
                tt(ta, xr_, twc[qi][d], ALU.mult, [kr, ck_], [ka])
                ptt(tb_, xi_, tws[qi][d], ALU.mult, [ki, sk_], [kb])
                tt(o_r, ta, tb_, ALU.subtract, [ka, kb], ["Xb%d%d0" % (qi, d)])
                tt(tc, xr_, tws[qi][d], ALU.mult, [kr, sk_], [kc_])
                ptt(td, xi_, twc[qi][d], ALU.mult, [ki, ck_], [kd])
                tt(o_i, tc, td, ALU.add, [kc_, kd], ["Xb%d%d1" % (qi, d)])

        def stageC(q):
            qi = q % 2
            q3 = q % 3
            wqi = wq[q3]
            wk = "wq%d" % q3
            ufk = "uf%d" % q3
            zi = (q // 4) % 2
            for two in range(2):
                gp = (2 * q + two) % 8
                rows = slice(two * 64, (two + 1) * 64)
                pi_ = 3 + two
                pk = "ps%d" % pi_
                po = psf(pi_)[:, 0:NJ]
                mm(po, wqi[:, 8 + two, :], uf[q3][:, two, :], True, False, [wk, ufk], [pk])
                for part in range(2):
                    mm(po[:, 1:NJ], wqi[rows, 4 + part, :], Xb[qi][0][part][rows, 0:NJ - 1], False, False,
                       [wk, "Xb%d0%d" % (qi, part)], [pk])
                for part in range(2):
                    mm(po[:, 0:NJ - 1], wqi[rows, 6 + part, :], Xb[qi][1][part][rows, 1:NJ], False, part == 1,
                       [wk, "Xb%d1%d" % (qi, part)], [pk])
                g_, s_ = gq[two], gs[two]
                gk, sk2 = "gq%d" % two, "gs%d" % two
                act(g_, po, AF.Square, [pk], [gk])
                act(g_, g_, AF.Identity, [gk, "one_c"], [gk], bias=one_c[:, 0:1], scale=0.044715)
                tt(g_, g_, po, ALU.mult, [gk, pk], [gk])
                act(s_, g_, AF.Sigmoid, [gk], [sk2], scale=1.5957691216057308)
                tt(zfold[zi][:, gp, :], s_, po, ALU.mult, [sk2, pk], ["zfold%d" % zi])
            if q % 4 == 3:
                chn = q // 4
                for s_i in range(8):
                    pi_ = 3 + s_i % 2
                    pk = "ps%d" % pi_
                    po = psf(pi_)[:, 0:NJ]
                    for gp in range(8):
                        mm(po, wide[:, s_i, (7 - gp) * 16:(7 - gp) * 16 + 128], zfold[zi][:, gp, :], gp == 0, gp == 7,
                           ["wide", "zfold%d" % zi], [pk])
                    tcopy("scalar", us5[:, chn, s_i:L:8], po, [pk], ["us5_%d" % chn])

        n_units = 12 * NTB
        unit_ctr = [0]

        def uhy_unit():
            k = unit_ctr[0]
            if k >= n_units:
                return
            unit_ctr[0] += 1
            mt = 4 + k // NTB
            tb = k % NTB
            wi = mt % 2
            wk = "wst2_%d" % wi
            if tb == 0:
                DMA(wst2[wi], win_v[:, :, mt * 128:(mt + 1) * 128], [], [wk], eng="gpsimd")
            pi_ = 5 + k % 3
            pk = "ps%d" % pi_
            po = psf(pi_)[:, 0:W]
            for kc in range(8):
                mm(po, wst2[wi][:, kc, :], hT[:, kc, tb * W:(tb + 1) * W], kc == 0, kc == 7, [wk, "hT"], [pk])
            tcopy("scalar", uhy[:, mt - 4, tb * W:(tb + 1) * W], po, [pk], ["uhy_%d" % (mt - 4)])

        per_q = -(-n_units // 16)
        tables(0)
        stageA(0)
        tables(1)
        stageA(1)
        stageB(0)
        for q in range(16):
            if (q + 2) % 4 == 0 and q + 2 < 16:
                s_major((q + 2) // 4)
            if q + 2 < 16:
                tables(q + 2)
                stageA(q + 2)
            for _ in range((per_q + 1) // 2):
                uhy_unit()
            if q + 1 < 16:
                stageB(q + 1)
            stageC(q)
            for _ in range(per_q // 2):
                uhy_unit()
        while unit_ctr[0] < n_units:
            uhy_unit()
        P.mark("b%d_p2_s5" % b)
        for oc in range(4):
            for tb in range(NTB):
                pi_ = (oc * NTB + tb) % 2
                pk = "ps%d" % pi_
                po = psf(pi_)[:, 0:W]
                for kc in range(4):
                    mm(po, gluw[:, kc, oc * 128:(oc + 1) * 128], us5[:, kc, tb * W:(tb + 1) * W], kc == 0, kc == 3,
                       ["gluw", "us5_%d" % kc], [pk])
                act(gsg, po, AF.Sigmoid, [pk, "glu_b"], ["gsg"], bias=glu_b[:, oc:oc + 1])
                tt(s5o[:, oc, tb * W:(tb + 1) * W], gsg, us5[:, oc, tb * W:(tb + 1) * W], ALU.mult,
                   ["gsg", "us5_%d" % oc], ["s5o"])
        P.barrier()
        P.mark("b%d_p2_glu" % b)

        at(E0)
        vg = AR.alloc([12, L], BF16)
        at(Z0)
        scts = [AR.alloc([L], F32) for _ in range(2)]
        vv = vg[:, 0:4, :].rearrange("p c (r m) -> p c r m", r=2)
        for m in range(12):
            u_ = uhy[:, m, :]
            uk = "uhy_%d" % m
            sct = scts[m % 2]
            act(sct, u_, AF.Identity, [uk, "cw", "cb"], ["sct%d" % (m % 2)], bias=cb[:, m:m + 1], scale=cw[:, m, 1:2])
            stt(sct[:, 1:L], u_[:, 0:L - 1], cw[:, m, 0:1], sct[:, 1:L], ALU.mult, ALU.add, [uk, "cw", "sct%d" % (m % 2)], ["sct%d" % (m % 2)])
            if m >= 4:
                stt(vg[:, m, 0:L - 1], u_[:, 1:L], cw[:, m, 2:3], sct[:, 0:L - 1], ALU.mult, ALU.add,
                    [uk, "cw", "sct%d" % (m % 2)], ["vg_%d" % m])
                tcopy("vector", vg[:, m, L - 1:L], sct[:, L - 1:L], ["sct%d" % (m % 2)], ["vg_%d" % m])
            else:
                stt(vv[:, m, 0, :], u_[:, 1:L:2], cw[:, m, 2:3], sct[:, 0:L:2], ALU.mult, ALU.add,
                    [uk, "cw", "sct%d" % (m % 2)], ["vg_%d" % m])
                stt(vv[:, m, 1, 0:LH - 1], u_[:, 2:L:2], cw[:, m, 2:3], sct[:, 1:L - 1:2], ALU.mult, ALU.add,
                    [uk, "cw", "sct%d" % (m % 2)], ["vg_%d" % m])
                tcopy("vector", vv[:, m, 1, LH - 1:LH], sct[:, L - 1:L], ["sct%d" % (m % 2)], ["vg_%d" % m])
        P.barrier()
        P.mark("b%d_p3_sconv" % b)
        at(A0)
        zTq = AR.alloc([2, KH, 512], BF16)
        YY = AR.alloc([4, KH, 512], BF16)
        fq = [AR.alloc([4, KH, 128], BF16) for _ in range(2)]
        hq = [AR.alloc([4, 512], F32) for _ in range(2)]
        fwd_end = AR.top
        assert AR.top <= D0, (AR.top, D0)
        at(Z0)
        cA = [AR.alloc([512], F32) for _ in range(6)]
        cM = [AR.alloc([512], F32) for _ in range(8)]
        assert AR.top <= arena_bytes
        at(fwd_end)
        gq = [AR.alloc([2, KH, 128], BF16) for _ in range(2)]
        yT = [AR.alloc([512], F32) for _ in range(2)]
        assert AR.top <= D0, (AR.top, D0)
        for o in range(2):
            vkeys = ["vg_%d" % cc for cc in range(4)]
            for r_ in range(2):
                for mh in range(KH):
                    i_ = (r_ * KH + mh) % 2
                    pk = "ps%d" % i_
                    pv = psb(i_)[:, 0:512].rearrange("p (a b) -> p a b", a=4)
                    for cc in range(4):
                        tr(pv[:, cc, :], vv[:, cc, r_, mh * 128:(mh + 1) * 128], idb, ["vg_%d" % cc, "idb"], [pk])
                    tcopy("scalar" if i_ else "vector", zTq[:, r_, mh, :], psb(i_)[:, 0:512], [pk], ["zTq"])
            P.mark("b%d_p3_c%d_T" % (b, o))
            for kt in range(KH):
                bb_ = kt % 2
                fk, hk = "fq%d" % bb_, "hq%d" % bb_
                DMA(fq[bb_], Fq_h.ap()[kt], [], [fk])
                DMA(hq[bb_], hs_h.ap()[kt][:, :, o * 512:(o + 1) * 512], ["hspec_%d_%d" % (kt, o)], [hk])
                pb4 = 4 * bb_
                for base_, Xs in ((0, (0, 1)), (1, (1,)), (2, (2, 3)), (3, (3,))):
                    n_mm = len(Xs) * KH
                    i_mm = 0
                    for X in Xs:
                        for mh in range(KH):
                            mm(psf(pb4 + base_), fq[bb_][:, X, mh, :], zTq[:, X % 2, mh, :], i_mm == 0, i_mm == n_mm - 1,
                               [fk, "zTq"], ["ps%d" % (pb4 + base_)])
                            i_mm += 1
                kA, kAo, kB, kBo = ["ps%d" % (pb4 + X) for X in range(4)]
                AoS, BoS, A_, A2_, B_, B2_ = cA
                tcopy("scalar", AoS, psf(pb4 + 1), [kAo], ["cA0"])
                tcopy("scalar", BoS, psf(pb4 + 3), [kBo], ["cA1"])
                stt(A2_, AoS, -2.0, psf(pb4 + 0), ALU.mult, ALU.add, [kA, "cA0"], ["cA3"])
                stt(B2_, BoS, 2.0, psf(pb4 + 2), ALU.mult, ALU.subtract, [kB, "cA1"], ["cA5"])
                Ha, Hb, Ha2, Hb2 = [hq[bb_][:, i, :] for i in range(4)]
                tt(cM[0], psf(pb4 + 0), Ha, ALU.mult, [kA, hk], ["cM0"])
                tt(cM[1], psf(pb4 + 2), Hb, ALU.mult, [kB, hk], ["cM1"])
                tt(cM[2], psf(pb4 + 0), Hb, ALU.mult, [kA, hk], ["cM2"])
                tt(cM[3], psf(pb4 + 2), Ha, ALU.mult, [kB, hk], ["cM3"])
                ptt(cM[4], A2_, Ha2, ALU.mult, ["cA3", hk], ["cM4"])
                ptt(cM[5], B2_, Hb2, ALU.mult, ["cA5", hk], ["cM5"])
                ptt(cM[6], A2_, Hb2, ALU.mult, ["cA3", hk], ["cM6"])
                ptt(cM[7], B2_, Ha2, ALU.mult, ["cA5", hk], ["cM7"])
                tt(cM[0], cM[0], cM[1], ALU.subtract, ["cM0", "cM1"], ["cM0"])
                tt(cM[2], cM[2], cM[3], ALU.add, ["cM2", "cM3"], ["cM2"])
                ptt(cM[4], cM[4], cM[5], ALU.subtract, ["cM4", "cM5"], ["cM4"])
                ptt(cM[6], cM[6], cM[7], ALU.add, ["cM6", "cM7"], ["cM6"])
                tt(YY[:, 0, kt, :], cM[0], cM[4], ALU.add, ["cM0", "cM4"], ["YY"])
                tt(YY[:, 2, kt, :], cM[0], cM[4], ALU.subtract, ["cM0", "cM4"], ["YY"])
                tt(YY[:, 1, kt, :], cM[2], cM[6], ALU.subtract, ["cM2", "cM6"], ["YY"])
                tt(YY[:, 3, kt, :], cM[2], cM[6], ALU.add, ["cM2", "cM6"], ["YY"])
            P.mark("b%d_p3_c%d_fwd" % (b, o))
            for r_ in range(2):
                for mt in range(KH):
                    it = r_ * KH + mt
                    bb_ = it % 2
                    gk = "gq%d" % bb_
                    DMA(gq[bb_], Gq_h.ap()[r_, mt], [], [gk])
                    pi_ = 6 + bb_
                    pk = "ps%d" % pi_
                    for cs_ in range(2):
                        for kt in range(KH):
                            mm(psf(pi_), gq[bb_][:, cs_, kt, :], YY[:, 2 * r_ + cs_, kt, :], cs_ == 0 and kt == 0,
                               cs_ == 1 and kt == KH - 1, [gk, "YY"], [pk])
                    tcopy("scalar", yT[bb_], psf(pi_), [pk], ["yT%d" % bb_])
                    pj = bb_
                    pkj = "ps%d" % pj
                    pvj = psf(pj).rearrange("p (a b) -> p a b", a=4)
                    for cc in range(4):
                        tr(pvj[:, cc, :], yT[bb_][:, cc * 128:(cc + 1) * 128], idf, ["yT%d" % bb_, "idf"], [pkj])
                    t0_ = 2 * mt * 128 + r_
                    gate = vg[:, 4 + 4 * o:8 + 4 * o, t0_:t0_ + 255:2]
                    if o == 0:
                        dst = vv[:, :, r_, mt * 128:(mt + 1) * 128]
                    else:
                        dst = vg[:, 0:4, t0_:t0_ + 255:2]
                    tt(dst, pvj, gate, ALU.mult, [pkj] + ["vg_%d" % (4 + 4 * o + cc) for cc in range(4)], vkeys)
            if o == 1:
                P.barrier()
            P.mark("b%d_p3_c%d_inv" % (b, o))

        z2 = vg[:, 0:4, :]
        DMA(hT, hts_h.ap(), ["hts"], ["hT"])
        at(B0)
        wab = AR.alloc([4, D], BF16)
        wbb = AR.alloc([4, D], BF16)
        DMA(wab, wa_h.ap().rearrange("(kc kl) n -> kl kc n", kl=128), [], ["wab"], eng="gpsimd")
        DMA(wbb, wb_h.ap().rearrange("(kc kl) n -> kl kc n", kl=128), [], ["wbb"], eng="gpsimd")
        mg = AR.alloc([8, L], BF16)
        wg_ = [[AR.alloc([8, 128], BF16) for _ in range(2)] for _ in range(2)]
        sg0 = AR.alloc([W], F32)
        sg1 = AR.alloc([W], F32)
        assert AR.top <= D0, (AR.top, D0)
        at(E0 + 4 * LR * 2)
        wo = AR.alloc([8, D], BF16)
        for dch in range(8):
            bb_ = dch % 2
            if dch == 2:
                DMA(wo, wout_h.ap().rearrange("(kc kl) n -> kl kc n", kl=128), [], ["wo"], eng="gpsimd")
            for gi in range(2):
                DMA(wg_[bb_][gi], win_v[:, :, 2048 + gi * 1024 + dch * 128:2048 + gi * 1024 + (dch + 1) * 128], [],
                    ["wg%d%d" % (bb_, gi)], eng="gpsimd")
            for tb in range(NTB):
                tsl = slice(tb * W, (tb + 1) * W)
                base = 4 * (tb % 2)
                pa, pb_, pg0, pg1 = base, base + 1, base + 2, base + 3
                for kc in range(4):
                    mm(psf(pa)[:, 0:W], wab[:, kc, dch * 128:(dch + 1) * 128], s5o[:, kc, tsl], kc == 0, kc == 3,
                       ["wab", "s5o"], ["ps%d" % pa])
                for kc in range(4):
                    mm(psf(pb_)[:, 0:W], wbb[:, kc, dch * 128:(dch + 1) * 128], z2[:, kc, tsl], kc == 0, kc == 3,
                       ["wbb"] + ["vg_%d" % cc for cc in range(4)], ["ps%d" % pb_])
                for gi, pg in ((0, pg0), (1, pg1)):
                    for kc in range(8):
                        mm(psf(pg)[:, 0:W], wg_[bb_][gi][:, kc, :], hT[:, kc, tsl], kc == 0, kc == 7,
                           ["wg%d%d" % (bb_, gi), "hT"], ["ps%d" % pg])
                act(sg0, psf(pg0)[:, 0:W], AF.Sigmoid, ["ps%d" % pg0], ["sg0"])
                act(sg1, psf(pg1)[:, 0:W], AF.Sigmoid, ["ps%d" % pg1], ["sg1"])
                tt(sg0, sg0, psf(pa)[:, 0:W], ALU.mult, ["sg0", "ps%d" % pa], ["sg0"])
                tt(sg1, sg1, psf(pb_)[:, 0:W], ALU.mult, ["sg1", "ps%d" % pb_], ["sg1"])
                tt(mg[:, dch, tsl], sg0, sg1, ALU.add, ["sg0", "sg1"], ["mg"])
        P.barrier()
        P.mark("b%d_p4_merge" % b)
        at(E0)
        rowg1 = AR.alloc([D], F32)
        rowA2 = AR.alloc([D], F32)
        rowB2 = AR.alloc([D], F32)
        load_row(rowg1, b, 2, "rowg1")
        load_row(rowB2, b, 3, "rowB2")
        load_row(rowA2, b, 4, "rowA2")
        rown = AR.alloc([D], F32)
        DMA(rown, bcast_rows(n2g_h, 0, D), [], ["rown"])
        stt(rowA2, rowA2, 1.0, rown, ALU.add, ALU.mult, ["rowA2", "rown"], ["rowA2"])
        assert AR.top <= E0 + 4 * LR * 2
        at(E0 + 4 * LR * 2 + 8 * D * 2)
        xt = [AR.alloc([D], F32) for _ in range(3)]
        x1 = [AR.alloc([D], F32) for _ in range(2)]
        sqs = [AR.alloc([D], F32) for _ in range(2)]
        sqm = [AR.alloc([D], F32) for _ in range(2)]
        hb_ = [AR.alloc([D], BF16) for _ in range(2)]
        ss = [AR.alloc([1], F32) for _ in range(2)]
        h2T = hT

        def p4_s0(t_):
            i3 = t_ % 3
            DMA(xt[i3], x_h.ap()[b, t_ * 128:(t_ + 1) * 128, :], [], ["xt%d" % i3])
            for hh in range(2):
                pi_ = 2 * i3 + hh
                for kc in range(8):
                    mm(psf(pi_), mg[:, kc, t_ * 128:(t_ + 1) * 128], wo[:, kc, hh * 512:(hh + 1) * 512], kc == 0, kc == 7,
                       ["mg", "wo"], ["ps%d" % pi_])

        def p4_s1(t_):
            i3 = t_ % 3
            i2 = t_ % 2
            xk = "xt%d" % i3
            x1k = "x1%d" % i2
            for hh in range(2):
                pi_ = 2 * i3 + hh
                cs = slice(hh * 512, (hh + 1) * 512)
                tt(x1[i2][:, cs], psf(pi_), rowg1[:, cs], ALU.mult, ["ps%d" % pi_, "rowg1"], [x1k])
            ptt(x1[i2], x1[i2], xt[i3], ALU.add, [x1k, xk], [x1k])
            DMA(x1s_h.ap()[t_ * 128:(t_ + 1) * 128, :], x1[i2], [x1k], ["x1s_%d" % t_])
            rms_rstd(x1[i2], x1k, sqs[i2], ss[i2], "b%d" % i2)
            stt(sqm[i2], x1[i2], ss[i2][:, 0:1], rowA2, ALU.mult, ALU.mult, [x1k, "ssb%d" % i2, "rowA2"], ["sqm%d" % i2])
            ptt(hb_[i2], sqm[i2], rowB2, ALU.add, ["sqm%d" % i2, "rowB2"], ["hb%d" % i2])

        def p4_s2(t_):
            i2 = t_ % 2
            pi_ = 6 + i2
            pk = "ps%d" % pi_
            pv = psb(pi_).rearrange("p (a b) -> p a b", a=8)
            for dc in range(8):
                tr(pv[:, dc, :], hb_[i2][:, dc * 128:(dc + 1) * 128], idb, ["hb%d" % i2, "idb"], [pk])
            tcopy("scalar", h2T[:, :, t_ * 128:(t_ + 1) * 128], pv, [pk], ["h2T"])

        p4_s0(0)
        if TT > 1:
            p4_s0(1)
        p4_s1(0)
        for t_ in range(TT):
            if t_ + 2 < TT:
                p4_s0(t_ + 2)
            if t_ + 1 < TT:
                p4_s1(t_ + 1)
            p4_s2(t_)
        P.barrier()
        P.mark("b%d_p4_wout" % b)

        at(B0)
        actT = AR.alloc([NF, L], BF16)
        at(B0 + NF * LR * 2)
        wd = AR.alloc([NF, D], BF16)
        wgu = [[AR.alloc([8, 256], BF16) for _ in range(2)] for _ in range(2)]
        sgl = AR.alloc([W], F32)
        rowg2 = AR.alloc([D], F32)
        rowfg = AR.alloc([D], F32)
        load_row(rowg2, b, 5, "rowg2")
        DMA(rowfg, bcast_rows(fg_h, 0, D), [], ["rowfg"])
        assert AR.top <= arena_bytes
        wgu_v = wgu_h.ap().rearrange("(kh kl) n -> kl kh n", kl=128)
        for f in range(NF):
            bb_ = (f // 2) % 2
            fo = (f % 2) * 128
            if f == 3:
                DMA(wd, wdn_h.ap().rearrange("(f p) n -> p f n", p=128), [], ["wd"], eng="gpsimd")
            if f % 2 == 0:
                DMA(wgu[bb_][0], wgu_v[:, :, f * 128:(f + 2) * 128], [], ["wgu%d0" % bb_], eng="gpsimd")
                DMA(wgu[bb_][1], wgu_v[:, :, DFF + f * 128:DFF + (f + 2) * 128], [], ["wgu%d1" % bb_], eng="gpsimd")
            for tb in range(NTB):
                tsl = slice(tb * W, (tb + 1) * W)
                base = 2 * ((f * NTB + tb) % 4)
                pg, pu = base, base + 1
                for kc in range(8):
                    mm(psf(pg)[:, 0:W], wgu[bb_][0][:, kc, fo:fo + 128], h2T[:, kc, tsl], kc == 0, kc == 7,
                       ["wgu%d0" % bb_, "h2T"], ["ps%d" % pg])
                for kc in range(8):
                    mm(psf(pu)[:, 0:W], wgu[bb_][1][:, kc, fo:fo + 128], h2T[:, kc, tsl], kc == 0, kc == 7,
                       ["wgu%d1" % bb_, "h2T"], ["ps%d" % pu])
                act(sgl, psf(pg)[:, 0:W], AF.Sigmoid, ["ps%d" % pg], ["sgl"])
                tt(sgl, sgl, psf(pg)[:, 0:W], ALU.mult, ["sgl", "ps%d" % pg], ["sgl"])
                tt(actT[:, f, tsl], sgl, psf(pu)[:, 0:W], ALU.mult, ["sgl", "ps%d" % pu], ["actT"])
        P.barrier()
        P.mark("b%d_p5_gu" % b)
        actT2 = actT
        at(A0)
        x1 = [AR.alloc([D], F32) for _ in range(2)]
        x2 = [AR.alloc([D], F32) for _ in range(2)]
        sqs = [AR.alloc([D], F32) for _ in range(2)]
        ss = [AR.alloc([1], F32) for _ in range(2)]
        assert AR.top <= B0, (AR.top, B0)
        for t_ in range(TT):
            i2 = t_ % 2
            x1k = "x1%d" % i2
            DMA(x1[i2], x1s_h.ap()[t_ * 128:(t_ + 1) * 128, :], ["x1s_%d" % t_], [x1k])
            p0, p1 = 2 * i2, 2 * i2 + 1
            for hh, pi_ in ((0, p0), (1, p1)):
                for f in range(NF):
                    mm(psf(pi_), actT2[:, f, t_ * 128:(t_ + 1) * 128], wd[:, f, hh * 512:(hh + 1) * 512], f == 0,
                       f == NF - 1, ["actT", "wd"], ["ps%d" % pi_])
            x2k = "x2%d" % i2
            for hh, pi_ in ((0, p0), (1, p1)):
                cs = slice(hh * 512, (hh + 1) * 512)
                tt(x2[i2][:, cs], psf(pi_), rowg2[:, cs], ALU.mult, ["ps%d" % pi_, "rowg2"], [x2k])
            ptt(x2[i2], x2[i2], x1[i2], ALU.add, [x2k, x1k], [x2k])
            rms_rstd(x2[i2], x2k, sqs[i2], ss[i2], "c%d" % i2)
            stt(x2[i2], x2[i2], ss[i2][:, 0:1], rowfg, ALU.mult, ALU.mult, [x2k, "ssc%d" % i2, "rowfg"], [x2k])
            DMA(out_h.ap()[b, t_ * 128:(t_ + 1) * 128, :], x2[i2], [x2k], ["out_%d_%d" % (b, t_)])
        P.barrier()
        P.mark("b%d_p5_down" % b)

    import os as _os
    if _os.environ.get("KMARKS"):
        import json as _json
        _json.dump(P.marks, open(_os.environ["KMARKS"], "w"))
    P.emit()
    st.close()
    return nc


def host_consts(L):
    N = 2 * L
    bf = ml_dtypes.bfloat16
    c = {}
    c["ident_f"] = np.eye(128, dtype=np.float32)
    c["ident_b"] = np.eye(128).astype(bf)
    wide = np.zeros((128, 8, 240), np.float32)
    for a in range(8):
        for ch in range(16):
            wide[a * 16 + ch, a, 112 + ch] = 1.0
    c["wide"] = wide.astype(bf)
    s_idx = np.arange(128) // 16
    c["mask_f"] = (s_idx[None, :] >= s_idx[:, None]).astype(np.float32)
    c["mask_b"] = (s_idx[:, None] >= s_idx[None, :]).astype(np.float32)
    t = np.arange(L, dtype=np.float64)
    t01 = t / max(L - 1, 1)
    bands = np.linspace(1e-4, 15.0, 16)
    ang = 2.0 * np.pi * t[:, None] * bands[None, :] / L
    feats = np.concatenate([t01[:, None], np.cos(ang), np.sin(ang)], axis=-1)
    c["featsT"] = np.ascontiguousarray(feats.T).astype(np.float32)
    c["negt01"] = np.ascontiguousarray((-t01).reshape(L // 128, 128).T).astype(np.float32)
    c["jidx"] = np.broadcast_to(np.arange(L // 8, dtype=np.float32)[None, :], (128, L // 8)).copy()
    LHn = L // 2
    KHn = LHn // 128
    k = np.arange(LHn, dtype=np.float64) + 0.5
    m = np.arange(LHn, dtype=np.float64)
    Fq = np.zeros((KHn, 128, 4, KHn, 128), np.float32)
    Gq = np.zeros((2, KHn, 128, 2, KHn, 128), np.float32)
    for r_ in range(2):
        th = 2.0 * np.pi * (2.0 * m + r_)[:, None] * k[None, :] / N
        for fi, fn in enumerate((np.cos, np.sin)):
            M = fn(th)
            X = 2 * fi + r_
            Fq[:, :, X, :, :] = M.reshape(KHn, 128, KHn, 128).transpose(2, 1, 0, 3)
            Gq[r_, :, :, fi, :, :] = (M.T * (2.0 / N)).reshape(KHn, 128, KHn, 128).transpose(2, 1, 0, 3)
    c["Fq"] = Fq.astype(bf)
    c["Gq"] = Gq.astype(bf)
    tq = np.zeros((128, 2, KHn), np.float32)
    for r_ in range(2):
        for mh in range(KHn):
            tq[:, r_, mh] = -t01[2 * (mh * 128 + np.arange(128)) + r_]
    c["negt01q"] = tq
    return c


def host_params(inp):
    f = np.float32
    p = {}
    for k_ in ["ada_w", "w_in", "s5_glu_w", "w_branch_a", "w_branch_b", "w_out", "ffn_w_gu", "ffn_w_down"]:
        p[k_] = np.ascontiguousarray(np.asarray(inp[k_], f)[0])
    p["ada_b"] = np.asarray(inp["ada_b"], f).reshape(1, -1)
    p["norm1_g"] = np.asarray(inp["norm1_g"], f).reshape(1, -1)
    p["norm2_g"] = np.asarray(inp["norm2_g"], f).reshape(1, -1)
    p["final_g"] = np.asarray(inp["final_g"], f).reshape(1, -1)

    def st_(a):
        return np.ascontiguousarray(np.asarray(a, f)[0].reshape(2, 16, 2, 64).transpose(2, 3, 0, 1).reshape(128, 32))

    p["lamre_t"] = st_(inp["s5_lam_re"])
    p["lamim_t"] = st_(inp["s5_lam_im"])
    ls = np.asarray(inp["s5_log_step"], f)[0].reshape(2, 16, 2).transpose(2, 0, 1)
    p["lstep_t"] = np.ascontiguousarray(np.broadcast_to(ls[:, None, :, :], (2, 64, 2, 16)).reshape(128, 32))

    def bt_(a):
        return np.ascontiguousarray(
            np.asarray(a, f)[0].reshape(2, 16, 2, 64, 16).transpose(2, 3, 0, 1, 4).reshape(128, 32, 16))

    def ct_(a):
        return np.ascontiguousarray(
            np.asarray(a, f)[0].reshape(2, 16, 2, 16, 64).transpose(2, 4, 0, 1, 3).reshape(128, 32, 16))

    p["bre_t"] = bt_(inp["s5_b_re"])
    p["bim_t"] = bt_(inp["s5_b_im"])
    p["cre_t"] = ct_(inp["s5_c_re"])
    p["cim_t"] = ct_(inp["s5_c_im"])
    dcol = np.asarray(inp["s5_d"], f)[0].reshape(32, 16).T
    p["dcol"] = np.ascontiguousarray(np.tile(dcol, (8, 1)))
    p["glu_b_col"] = np.ascontiguousarray(np.asarray(inp["s5_glu_b"], f)[0].reshape(4, 128).T)
    p["conv_w_col"] = np.ascontiguousarray(np.asarray(inp["hy_conv_w"], f)[0].reshape(3, 12, 128).transpose(2, 1, 0))
    p["conv_b_col"] = np.ascontiguousarray(np.asarray(inp["hy_conv_b"], f)[0].reshape(12, 128).T)
    p["hy_w1"] = np.ascontiguousarray(np.asarray(inp["hy_ffn_w1"], f)[0])
    p["hy_b1"] = np.asarray(inp["hy_ffn_b1"], f)[0].reshape(64, 1).copy()
    p["hy_w2"] = np.ascontiguousarray(np.asarray(inp["hy_ffn_w2"], f)[0])
    p["hy_b2"] = np.asarray(inp["hy_ffn_b2"], f)[0].reshape(64, 1).copy()
    p["hy_w3"] = np.ascontiguousarray(np.asarray(inp["hy_ffn_w3"], f)[0])
    p["hy_b3"] = np.asarray(inp["hy_ffn_b3"], f)[0].reshape(1, 2048).copy()
    p["hy_freq"] = np.asarray(inp["hy_freq"], f)[0].reshape(64, 1).copy()
    p["hy_decay"] = np.asarray(inp["hy_decay"], f)[0].reshape(1, 2048).copy()
    p["hy_bias"] = np.asarray(inp["hy_bias"], f)[0].reshape(1, 1024).copy()
    return p


def core_inputs(inp, params, consts, b0, NB):
    m = dict(params)
    m.update(consts)
    x = np.asarray(inp["x"], np.float32)
    c = np.asarray(inp["c"], np.float32)
    m["x"] = np.ascontiguousarray(x[b0:b0 + NB])
    m["cT"] = np.ascontiguousarray(c[b0:b0 + NB].reshape(NB, 8, 128).transpose(2, 1, 0))
    return m


_CACHE = {}


def kernel(**inputs):
    x = np.asarray(inputs["x"])
    B, L, _ = x.shape
    ncores = 8
    NB = B // ncores
    key = (L, NB)
    if key not in _CACHE:
        nc = bass.Bass("TRN2", target_bir_lowering=False)
        build(nc, L, NB)
        _CACHE[key] = (nc, host_consts(L))
    nc, consts = _CACHE[key]
    params = host_params(inputs)
    in_maps = [core_inputs(inputs, params, consts, i * NB, NB) for i in range(ncores)]
    res = run_bass_kernel_spmd(nc, in_maps, core_ids=list(range(ncores)))
    outs = [np.asarray(r["out"], np.float32) for r in res.results]
    return np.concatenate(outs, axis=0)
```
